# Optimizing a Trainium2 kernel written in Bass

```python
import jax, jax.numpy as jnp
from jax import lax
import numpy as np

D_MODEL = 2048
BATCH = 2
SEQ = 4096
DEPTH = 1

HEAD_DIM = 128
N_HEADS = D_MODEL // HEAD_DIM
NSA_HEADS = N_HEADS // 2
NSA_KV_HEADS = 2
NSA_GROUP = NSA_HEADS // NSA_KV_HEADS
MOBA_HEADS = N_HEADS - NSA_HEADS
CMP_BLOCK = 32
CMP_STRIDE = 16
SLC_BLOCK = 64
SLC_TOPK = 16
WINDOW = 512
FORCE_BONUS = 1e4
Q_BLOCK = 128
MOBA_BLOCK = 256
MOBA_TOPK = 3
MOBA_Q_CHUNK = 32
ROPE_THETA = 500000.0
ROPE_DIMS = HEAD_DIM // 4
FFN_HIDDEN = -(-8 * D_MODEL // (3 * 256)) * 256

NSA_Q_W = NSA_HEADS * HEAD_DIM
NSA_KV_W = NSA_KV_HEADS * HEAD_DIM
NSA_GATE_W = NSA_HEADS * 3
MOBA_W = MOBA_HEADS * HEAD_DIM
IN_SIZES = (NSA_Q_W,) + (NSA_KV_W,) * 6 + (NSA_GATE_W,) + (MOBA_W,) * 3
IN_WIDTH = sum(IN_SIZES)
IN_OFFSETS = tuple(int(v) for v in np.cumsum(IN_SIZES)[:-1])

kernel_name = 'hymba_nsa_moba_adaln_block'


def rms_norm(x, eps=1e-6):
    xf = x.astype(jnp.float32)
    return (xf * lax.rsqrt(jnp.mean(xf * xf, axis=-1, keepdims=True) + eps)).astype(x.dtype)


def head_rms(x, g):
    return rms_norm(x) * g


def split_heads(x, n):
    b, t, _ = x.shape
    return x.reshape(b, t, n, HEAD_DIM).transpose(0, 2, 1, 3)


def rope_tables(pos):
    inv = ROPE_THETA ** (-jnp.arange(0, ROPE_DIMS, 2, dtype=jnp.float32) / ROPE_DIMS)
    ang = pos.astype(jnp.float32)[:, None] * inv[None, :]
    return jnp.cos(ang), jnp.sin(ang)


def apply_rope(x, cos, sin):
    half = ROPE_DIMS // 2
    x1, x2, rest = x[..., :half], x[..., half:ROPE_DIMS], x[..., ROPE_DIMS:]
    c, s = cos.astype(x.dtype), sin.astype(x.dtype)
    return jnp.concatenate([x1 * c - x2 * s, x2 * c + x1 * s, rest], axis=-1)


def masked_softmax(s, mask):
    s = jnp.where(mask, s.astype(jnp.float32), -1e30)
    p = jax.nn.softmax(s, axis=-1)
    return jnp.where(mask, p, 0.0)


def gather_blocks(blocks, idx):
    return jax.vmap(jax.vmap(lambda b_, i_: b_[i_]))(blocks, idx)


def seq_chunks(a, axis, size):
    shp = a.shape
    a = a.reshape(shp[:axis] + (shp[axis] // size, size) + shp[axis + 1:])
    return jnp.moveaxis(a, axis, 0)


def compress(kv, pe, w1, w2):
    t = kv.shape[2]
    nc = (t - CMP_BLOCK) // CMP_STRIDE + 1
    idx = jnp.arange(nc)[:, None] * CMP_STRIDE + jnp.arange(CMP_BLOCK)[None, :]
    blocks = kv[:, :, idx] + pe
    flat = blocks.reshape(blocks.shape[:3] + (CMP_BLOCK * HEAD_DIM,))
    return jax.nn.silu(flat @ w1) @ w2


def nsa_attention(q, k_cmp, v_cmp, k_slc, v_slc, k_win, v_win, gate_logits, q_g, k_g,
                  cmp_pe_k, cmp_w1_k, cmp_w2_k, cmp_pe_v, cmp_w1_v, cmp_w2_v, cos, sin):
    B, T = q.shape[:2]
    G, R = NSA_KV_HEADS, NSA_GROUP
    scale = HEAD_DIM ** -0.5
    pos = jnp.arange(T)
    q = apply_rope(head_rms(split_heads(q, NSA_HEADS), q_g), cos, sin)
    q = q.reshape(B, G, R, T, HEAD_DIM)

    kc = compress(split_heads(k_cmp, G), cmp_pe_k, cmp_w1_k, cmp_w2_k)
    vc = compress(split_heads(v_cmp, G), cmp_pe_v, cmp_w1_v, cmp_w2_v)
    nc = kc.shape[2]
    cmp_end = jnp.arange(nc) * CMP_STRIDE + CMP_BLOCK - 1
    kc = apply_rope(head_rms(kc, k_g[0]), cos[cmp_end], sin[cmp_end])
    s = jnp.einsum('bgrtd,bgcd->bgrtc', q, kc) * scale
    p_cmp = masked_softmax(s, cmp_end[None, :] <= pos[:, None])
    o_cmp = jnp.einsum('bgrtc,bgcd->bgrtd', p_cmp.astype(vc.dtype), vc)

    nsb = T // SLC_BLOCK
    cs = jnp.arange(nc) * CMP_STRIDE
    sb = jnp.arange(nsb) * SLC_BLOCK
    overlap = ((cs[:, None] < sb[None, :] + SLC_BLOCK) &
               (cs[:, None] + CMP_BLOCK > sb[None, :])).astype(jnp.float32)
    imp = jnp.einsum('bgtc,cn->bgtn', p_cmp.sum(axis=2), overlap)
    cur = pos // SLC_BLOCK
    blk = jnp.arange(nsb)[None, :]
    forced = (blk == 0) | (blk == cur[:, None]) | (blk == cur[:, None] - 1)
    valid = blk <= cur[:, None]
    score = jnp.where(valid, imp + FORCE_BONUS * forced, -1e30)
    n_sel = min(SLC_TOPK, nsb)
    _, sel_idx = lax.top_k(score, n_sel)
    sel_valid = sel_idx <= cur[:, None]

    ks = apply_rope(head_rms(split_heads(k_slc, G), k_g[1]), cos, sin)
    vs = split_heads(v_slc, G)
    ks_blocks = ks.reshape(B, G, nsb, SLC_BLOCK, HEAD_DIM)
    vs_blocks = vs.reshape(B, G, nsb, SLC_BLOCK, HEAD_DIM)
    n_keys = n_sel * SLC_BLOCK

    def sel_chunk(args):
        qc, ic, vld, tc = args
        kg = gather_blocks(ks_blocks, ic).reshape(B, G, Q_BLOCK, n_keys, HEAD_DIM)
        vg = gather_blocks(vs_blocks, ic).reshape(B, G, Q_BLOCK, n_keys, HEAD_DIM)
        kpos = (ic[..., None] * SLC_BLOCK + jnp.arange(SLC_BLOCK)).reshape(B, G, Q_BLOCK, n_keys)
        mask = jnp.repeat(vld, SLC_BLOCK, axis=-1) & (kpos <= tc[:, None])
        sc = jnp.einsum('bgrqd,bgqnd->bgrqn', qc, kg) * scale
        p = masked_softmax(sc, mask[:, :, None]).astype(vg.dtype)
        return jnp.einsum('bgrqn,bgqnd->bgrqd', p, vg)

    o_slc = lax.map(sel_chunk, (seq_chunks(q, 3, Q_BLOCK), seq_chunks(sel_idx, 2, Q_BLOCK),
                                seq_chunks(sel_valid, 2, Q_BLOCK), pos.reshape(-1, Q_BLOCK)))
    o_slc = jnp.moveaxis(o_slc, 0, 3).reshape(B, G, R, T, HEAD_DIM)

    kw = apply_rope(head_rms(split_heads(k_win, G), k_g[2]), cos, sin)
    vw = split_heads(v_win, G)
    nqb = T // Q_BLOCK
    span = WINDOW + Q_BLOCK
    win_idx = jnp.arange(nqb)[:, None] * Q_BLOCK + jnp.arange(span)[None, :]
    pad = ((0, 0), (0, 0), (WINDOW, 0), (0, 0))
    kwb = jnp.pad(kw, pad)[:, :, win_idx]
    vwb = jnp.pad(vw, pad)[:, :, win_idx]
    key_pos = win_idx - WINDOW
    q_pos = pos.reshape(nqb, Q_BLOCK)
    diff = q_pos[:, :, None] - key_pos[:, None, :]
    wmask = (key_pos[:, None, :] >= 0) & (diff >= 0) & (diff < WINDOW)
    qb = q.reshape(B, G, R, nqb, Q_BLOCK, HEAD_DIM)
    sw = jnp.einsum('bgrnqd,bgnkd->bgrnqk', qb, kwb) * scale
    pw = masked_softmax(sw, wmask).astype(vwb.dtype)
    o_win = jnp.einsum('bgrnqk,bgnkd->bgrnqd', pw, vwb).reshape(B, G, R, T, HEAD_DIM)

    g = jax.nn.sigmoid(gate_logits.astype(jnp.float32)).astype(q.dtype)
    g = g.reshape(B, T, G, R, 3).transpose(4, 0, 2, 3, 1)[..., None]
    o = g[0] * o_cmp + g[1] * o_slc + g[2] * o_win
    return o.reshape(B, NSA_HEADS, T, HEAD_DIM)


def moba_attention(q, k, v, q_g, k_g, cos, sin):
    B, T = q.shape[:2]
    H = MOBA_HEADS
    scale = HEAD_DIM ** -0.5
    q = apply_rope(head_rms(split_heads(q, H), q_g), cos, sin)
    k = apply_rope(head_rms(split_heads(k, H), k_g), cos, sin)
    v = split_heads(v, H)
    tp = -(-T // MOBA_BLOCK) * MOBA_BLOCK
    pad = ((0, 0), (0, 0), (0, tp - T), (0, 0))
    q, k, v = jnp.pad(q, pad), jnp.pad(k, pad), jnp.pad(v, pad)
    nb = tp // MOBA_BLOCK
    k_blocks = k.reshape(B, H, nb, MOBA_BLOCK, HEAD_DIM)
    v_blocks = v.reshape(B, H, nb, MOBA_BLOCK, HEAD_DIM)
    pos = jnp.arange(tp)
    cur = pos // MOBA_BLOCK

    gate = jnp.einsum('bhtd,bhnd->bhtn', q, jnp.mean(k_blocks, axis=3)).astype(jnp.float32)
    past = jnp.arange(nb)[None, :] < cur[:, None]
    n_sel = min(MOBA_TOPK, nb)
    _, sel_idx = lax.top_k(jnp.where(past, gate, -1e30), n_sel)
    sel_valid = sel_idx < cur[:, None]
    n_keys = n_sel * MOBA_BLOCK
    n_ch = tp // MOBA_Q_CHUNK
    own = (jnp.arange(n_ch) * MOBA_Q_CHUNK) // MOBA_BLOCK

    def chunk(args):
        qc, ic, vld, tc, ob = args
        kg = gather_blocks(k_blocks, ic).reshape(B, H, MOBA_Q_CHUNK, n_keys, HEAD_DIM)
        vg = gather_blocks(v_blocks, ic).reshape(B, H, MOBA_Q_CHUNK, n_keys, HEAD_DIM)
        k_own = lax.dynamic_index_in_dim(k_blocks, ob, axis=2, keepdims=False)
        v_own = lax.dynamic_index_in_dim(v_blocks, ob, axis=2, keepdims=False)
        s_sel = jnp.einsum('bhqd,bhqnd->bhqn', qc, kg)
        s_own = jnp.einsum('bhqd,bhmd->bhqm', qc, k_own)
        m_sel = jnp.repeat(vld, MOBA_BLOCK, axis=-1)
        m_own = jnp.broadcast_to(ob * MOBA_BLOCK + jnp.arange(MOBA_BLOCK) <= tc[:, None], s_own.shape)
        p = masked_softmax(jnp.concatenate([s_sel, s_own], axis=-1) * scale,
                           jnp.concatenate([m_sel, m_own], axis=-1)).astype(v_own.dtype)
        return (jnp.einsum('bhqn,bhqnd->bhqd', p[..., :n_keys], vg) +
                jnp.einsum('bhqm,bhmd->bhqd', p[..., n_keys:], v_own))

    o = lax.map(chunk, (seq_chunks(q, 2, MOBA_Q_CHUNK), seq_chunks(sel_idx, 2, MOBA_Q_CHUNK),
                        seq_chunks(sel_valid, 2, MOBA_Q_CHUNK), pos.reshape(n_ch, MOBA_Q_CHUNK), own))
    o = jnp.moveaxis(o, 0, 2).reshape(B, H, tp, HEAD_DIM)
    return o[:, :, :T]


def setup_inputs(seed: int = 0) -> dict:
    key = jax.random.key(seed)
    ks = jax.random.split(key, 20)
    f32 = jnp.float32
    L = DEPTH
    d_cmp = CMP_BLOCK * HEAD_DIM

    def nrm(k, shape, scale):
        return jax.random.normal(k, shape, f32) * scale

    def gain(k, shape):
        return 1.0 + 0.1 * jax.random.normal(k, shape, f32)

    return {
        'x': nrm(ks[0], (BATCH, SEQ, D_MODEL), 1.0),
        'c': nrm(ks[1], (BATCH, D_MODEL), 1.0),
        'w_ada': nrm(ks[2], (L, D_MODEL, 6 * D_MODEL), D_MODEL ** -0.5),
        'b_ada': nrm(ks[3], (L, 6 * D_MODEL), 0.02),
        'w_in': nrm(ks[4], (L, D_MODEL, IN_WIDTH), D_MODEL ** -0.5),
        'nsa_q_norm': gain(ks[5], (L, HEAD_DIM)),
        'nsa_k_norm': gain(ks[6], (L, 3, HEAD_DIM)),
        'moba_q_norm': gain(ks[7], (L, HEAD_DIM)),
        'moba_k_norm': gain(ks[8], (L, HEAD_DIM)),
        'cmp_pe_k': nrm(ks[9], (L, CMP_BLOCK, HEAD_DIM), 0.1),
        'cmp_w1_k': nrm(ks[10], (L, d_cmp, HEAD_DIM), d_cmp ** -0.5),
        'cmp_w2_k': nrm(ks[11], (L, HEAD_DIM, HEAD_DIM), HEAD_DIM ** -0.5),
        'cmp_pe_v': nrm(ks[12], (L, CMP_BLOCK, HEAD_DIM), 0.1),
        'cmp_w1_v': nrm(ks[13], (L, d_cmp, HEAD_DIM), d_cmp ** -0.5),
        'cmp_w2_v': nrm(ks[14], (L, HEAD_DIM, HEAD_DIM), HEAD_DIM ** -0.5),
        'out_norm': gain(ks[15], (L, D_MODEL)),
        'w_out': nrm(ks[16], (L, D_MODEL, D_MODEL), D_MODEL ** -0.5),
        'w_ffn_in': nrm(ks[17], (L, D_MODEL, 2 * FFN_HIDDEN), D_MODEL ** -0.5),
        'w_ffn_out': nrm(ks[18], (L, FFN_HIDDEN, D_MODEL), FFN_HIDDEN ** -0.5),
    }


def reference(x, c, w_ada, b_ada, w_in, nsa_q_norm, nsa_k_norm, moba_q_norm, moba_k_norm,
              cmp_pe_k, cmp_w1_k, cmp_w2_k, cmp_pe_v, cmp_w1_v, cmp_w2_v,
              out_norm, w_out, w_ffn_in, w_ffn_out):
    B, T, _ = x.shape
    cos, sin = rope_tables(jnp.arange(T))
    for l in range(DEPTH):
        mod = jax.nn.silu(c) @ w_ada[l] + b_ada[l]
        sh_a, sc_a, g_a, sh_f, sc_f, g_f = [m[:, None, :] for m in jnp.split(mod, 6, axis=-1)]

        h = rms_norm(x) * (1.0 + sc_a) + sh_a
        parts = jnp.split(h @ w_in[l], IN_OFFSETS, axis=-1)
        o_nsa = nsa_attention(parts[0], parts[1], parts[2], parts[3], parts[4], parts[5], parts[6],
                              parts[7], nsa_q_norm[l], nsa_k_norm[l],
                              cmp_pe_k[l], cmp_w1_k[l], cmp_w2_k[l],
                              cmp_pe_v[l], cmp_w1_v[l], cmp_w2_v[l], cos, sin)
        o_moba = moba_attention(parts[8], parts[9], parts[10], moba_q_norm[l], moba_k_norm[l], cos, sin)
        o = jnp.concatenate([o_nsa, o_moba], axis=1)
        o = rms_norm(o) * out_norm[l].reshape(N_HEADS, 1, HEAD_DIM)
        o = o.transpose(0, 2, 1, 3).reshape(B, T, D_MODEL)
        x = x + g_a * (o @ w_out[l])

        h = rms_norm(x) * (1.0 + sc_f) + sh_f
        gate, up = jnp.split(h @ w_ffn_in[l], 2, axis=-1)
        x = x + g_f * ((jax.nn.silu(gate) * up) @ w_ffn_out[l])
    return x
```

```python
import os
import numpy as np
import ml_dtypes
from contextlib import ExitStack
import concourse.bass as bass
import concourse.mybir as mybir
from concourse.bass_utils import run_bass_kernel_spmd

F32 = mybir.dt.float32
BF16 = mybir.dt.bfloat16
ALU = mybir.AluOpType
AF = mybir.ActivationFunctionType
AX = mybir.AxisListType

T = 4096
D = 2048
NT = 32
NO = 8
HID = 5632
NHC = HID // 128
SCALE = 128.0 ** -0.5
NEG = -30000.0
SAME_ENG_SYNC = os.environ.get("K_SES", "1") == "1"
NPHASE = int(os.environ.get("K_NPHASE", "99"))
KDEBUG = os.environ.get("K_DEBUG", "0") == "1"


class Buf:
    __slots__ = ("name", "lw", "rd")

    def __init__(self, name):
        self.name = name
        self.lw = None
        self.rd = []


class Op:
    __slots__ = ("eng", "fn", "r", "w", "deps", "signal", "tok", "dma")

    def __init__(self, eng, fn, r, w, dma=False):
        self.eng, self.fn, self.r, self.w = eng, fn, r, w
        self.deps = set()
        self.signal = False
        self.tok = None
        self.dma = dma


class Sched:
    NSLOT = 8

    def __init__(self, nc, es):
        self.nc = nc
        self.h = {"pe": nc.tensor, "act": nc.scalar, "dve": nc.vector, "pool": nc.gpsimd, "sp": nc.sync}
        self.sem = {}
        self.cnt = {}
        for e in ("pe", "act", "dve", "pool"):
            self.sem[e] = es.enter_context(nc.semaphore("s_" + e))
            self.cnt[e] = 0
        self.dq = {}
        for q in ("sp", "pool"):
            for s in range(self.NSLOT):
                k = ("dma", q, s)
                self.sem[k] = es.enter_context(nc.semaphore("d_%s%d" % (q, s)))
                self.cnt[k] = 0
            self.dq[q] = 0
        self.waited = {e: {} for e in self.h}
        self.ops = []
        self.bufs = []
        self.n_inst = 0

    def buf(self, name):
        b = Buf(name)
        self.bufs.append(b)
        return b

    def bufs_n(self, name, n):
        return [self.buf("%s%d" % (name, i)) for i in range(n)]

    def op(self, eng, fn, r=(), w=()):
        self.ops.append(Op(eng, fn, tuple(r), tuple(w)))

    def dma(self, q, out, in_, r=(), w=()):
        self.ops.append(Op(q, (out, in_), tuple(r), tuple(w), dma=True))

    def _wait(self, eng, key, val):
        wd = self.waited[eng]
        if wd.get(key, 0) >= val:
            return
        self.h[eng].wait_ge(self.sem[key], val)
        wd[key] = val
        self.n_inst += 1

    def flush(self, barrier=True):
        ops = self.ops
        for i, o in enumerate(ops):
            for b in o.r:
                if b.lw is not None:
                    o.deps.add(b.lw)
            for b in o.w:
                if b.lw is not None:
                    o.deps.add(b.lw)
                o.deps.update(b.rd)
            o.deps.discard(i)
            for b in o.r:
                if not o.dma:
                    b.rd = [x for x in b.rd if ops[x].dma or ops[x].eng != o.eng]
                b.rd.append(i)
            for b in o.w:
                b.lw = i
                b.rd = []
        for i, o in enumerate(ops):
            keep = set()
            for d in o.deps:
                p = ops[d]
                if (not p.dma) and p.eng == o.eng and (not o.dma):
                    if p.eng == "pe" or not SAME_ENG_SYNC:
                        continue
                p.signal = True
                keep.add(d)
            o.deps = keep
        last = {}
        for i, o in enumerate(ops):
            if not o.dma:
                last[o.eng] = i
        if barrier:
            for e, i in last.items():
                ops[i].signal = True
        for i, o in enumerate(ops):
            for d in sorted(o.deps):
                k, v = ops[d].tok
                self._wait(o.eng, k, v)
            if o.dma:
                q = o.eng
                s = self.dq[q] % self.NSLOT
                self.dq[q] += 1
                k = ("dma", q, s)
                if self.cnt[k] > 0:
                    self._wait(q, k, self.cnt[k])
                out, in_ = o.fn
                ins = self.h[q].dma_start(out=out, in_=in_)
                self.cnt[k] += 16
                ins.then_inc(self.sem[k], 16)
                o.tok = (k, self.cnt[k])
            else:
                ins = o.fn(self.h[o.eng])
                if o.signal:
                    self.cnt[o.eng] += 1
                    ins.then_inc(self.sem[o.eng], 1)
                    o.tok = (o.eng, self.cnt[o.eng])
            self.n_inst += 1
        if barrier:
            for e in self.h:
                for k in self.sem:
                    if k == e:
                        continue
                    if self.cnt[k] > 0:
                        self._wait(e, k, self.cnt[k])
        self.ops = []
        for b in self.bufs:
            b.lw = None
            b.rd = []
        self.bufs = [b for b in self.bufs if getattr(b, "name", "").startswith("G_")]


def _bf(a):
    return np.ascontiguousarray(a.astype(ml_dtypes.bfloat16))


def _rope_tab(pos):
    inv = (500000.0 ** (-np.arange(0, 32, 2, dtype=np.float32) / np.float32(32))).astype(np.float32)
    ang = pos.astype(np.float32)[:, None] * inv[None, :]
    return np.concatenate([np.cos(ang), np.sin(ang)], axis=1).astype(np.float32)


def _consts(j):
    c = {}
    own_tok = np.concatenate([np.arange(128 * (4 * i + j), 128 * (4 * i + j) + 128) for i in range(NO)])
    c["own_tok"] = own_tok
    cs_full = _rope_tab(np.arange(T))
    c["cs_full"] = np.ascontiguousarray(cs_full.reshape(NT, 128, 32).transpose(1, 0, 2))
    c["cs_own"] = np.ascontiguousarray(cs_full[own_tok].reshape(NO, 128, 32).transpose(1, 0, 2))
    cmp_end = np.arange(256) * 16 + 31
    cmp_end[255] = 0
    cs_cmp = _rope_tab(cmp_end)
    c["cs_cmp"] = np.ascontiguousarray(cs_cmp.reshape(2, 128, 32).transpose(1, 0, 2))
    cmp_end = np.arange(256) * 16 + 31
    kk = np.arange(128)[:, None]
    tt = np.arange(128)[None, :]
    cm = np.zeros((128, NO, 2, 128), np.float32)
    for i in range(NO):
        qb = 4 * i + j
        for ct in range(2):
            cidx = ct * 128 + kk
            vis = (cmp_end[cidx] <= (128 * qb + tt)) & (cidx < 255)
            cm[:, i, ct, :] = np.where(vis, 0.0, NEG)
    c["cmpmask"] = _bf(cm)
    cmk = np.zeros((128, 4, 128), np.float32)
    for jj in range(4):
        if jj < j:
            cmk[:, jj, :] = 0.0
        elif jj == j:
            cmk[:, jj, :] = np.where(kk <= tt, 0.0, NEG)
        else:
            cmk[:, jj, :] = NEG
    c["cmask"] = _bf(cmk)
    wm = np.zeros((128, 8, 128), np.float32)
    for w in range(8):
        dl = w - 4 - j
        if dl == 0:
            wm[:, w, :] = np.where(kk <= tt, 0.0, NEG)
        elif dl in (-1, -2, -3):
            wm[:, w, :] = 0.0
        elif dl == -4:
            wm[:, w, :] = np.where(kk > tt, 0.0, NEG)
        else:
            wm[:, w, :] = NEG
    c["wmask"] = _bf(wm)
    am = np.zeros((128, NO, 64), np.float32)
    amm = np.zeros((128, NO, 16), np.float32)
    for i in range(NO):
        qb = 4 * i + j
        pos = 128 * qb + np.arange(128)
        cur = pos // 64
        blk = np.arange(64)[None, :]
        forced = (blk == 0) | (blk == cur[:, None]) | (blk == cur[:, None] - 1)
        valid = blk <= cur[:, None]
        am[:, i, :] = np.where(valid, np.where(forced, 8.0, 0.0), -1e30)
        curm = pos // 256
        blkm = np.arange(16)[None, :]
        amm[:, i, :] = np.where(blkm < curm[:, None], 0.0, np.where(blkm == curm[:, None], 1e30, -1e30))
    c["addsel"] = am
    c["addmoba"] = amm
    xs = np.zeros((64, T), np.float32)
    xs[np.arange(T) // 64, np.arange(T)] = 1.0
    c["xsel"] = _bf(xs)
    xm = np.zeros((16, T), np.float32)
    xm[np.arange(T) // 256, np.arange(T)] = 1.0
    c["xmoba"] = _bf(xm)
    c["ident"] = _bf(np.eye(128, dtype=np.float32))
    cs_ = np.arange(256) * 16
    sb = np.arange(64) * 64
    ov = ((cs_[:, None] < sb[None, :] + 64) & (cs_[:, None] + 32 > sb[None, :])).astype(np.float32)
    oa = np.concatenate([np.ones((256, 1), np.float32), ov], axis=1)
    oa[255] = 0.0
    c["ovaug"] = _bf(oa.reshape(2, 128, 65).transpose(1, 0, 2))
    return c


def build_nc():
    nc = bass.Bass("TRN2", target_bir_lowering=False)

    uid = [0]

    def sbt(name, shape, dt):
        uid[0] += 1
        return nc.sbuf_tensor("%s_%d" % (name, uid[0]), shape, dt)

    def din(name, shape, dt=F32):
        return nc.dram_tensor(name, list(shape), dt, kind="ExternalInput").ap()

    def dscr(name, shape, dt):
        kind = "ExternalOutput" if KDEBUG else "Internal"
        return nc.dram_tensor(name, list(shape), dt, kind=kind).ap()

    x_full = din("x_full", [T, D])
    x_own = din("x_own", [1024, D])
    c_col = din("c_col", [128, 16])
    w_ada = din("w_ada", [D, 6 * D])
    b_ada = din("b_ada", [1, 6 * D])
    w_in = din("w_in", [D, 5656])
    g_nq = din("g_nq", [128, 128])
    g_nk = din("g_nk", [128, 3, 128])
    g_mq = din("g_mq", [128, 128])
    g_mk = din("g_mk", [128, 128])
    pe_k = din("pe_k", [32, 128])
    w1_k = din("w1_k", [4096, 128])
    w2_k = din("w2_k", [128, 128])
    pe_v = din("pe_v", [32, 128])
    w1_v = din("w1_v", [4096, 128])
    w2_v = din("w2_v", [128, 128])
    g_out = din("g_out", [128, D])
    w_out = din("w_out", [D, D])
    w_fi = din("w_fi", [D, 2 * HID])
    w_fo = din("w_fo", [HID, D])
    cs_full = din("cs_full", [128, NT, 32])
    cs_own = din("cs_own", [128, NO, 32])
    cs_cmp = din("cs_cmp", [128, 2, 32])
    cmpmask_d = din("cmpmask", [128, NO, 2, 128], BF16)
    cmask_d = din("cmask", [128, 4, 128], BF16)
    wmask_d = din("wmask", [128, 8, 128], BF16)
    addsel_d = din("addsel", [128, NO, 64])
    addmoba_d = din("addmoba", [128, NO, 16])
    xsel_d = din("xsel", [64, T], BF16)
    xmoba_d = din("xmoba", [16, T], BF16)
    ident_d = din("ident", [128, 128], BF16)
    ident32_d = din("ident32", [32, 32])
    ovaug_d = din("ovaug", [128, 2, 65], BF16)
    y_out = nc.dram_tensor("y_out", [1024, D], F32, kind="ExternalOutput").ap()

    modscr = dscr("modscr", [128, 6 * D], F32)
    KT = dscr("KT", [16, 128, T], BF16)
    VV = dscr("VV", [12, T, 128], BF16)
    QT = dscr("QT", [16, 128, 1024], BF16)
    OT = dscr("OT", [16, 128, 1024], BF16)
    GS = dscr("GS", [128, NO, 24], F32)

    with ExitStack() as es:
        E = es.enter_context
        S = Sched(nc, es)
        pf, pb, G_pf, G_pb = [], [], [], []
        G_pbh = [None]

        def psum_alloc(P, nf=6, nb=2):
            uid[0] += 1
            pf[:] = [P(nc.psum_tensor("pf%d_%d" % (i, uid[0]), [128, 512], F32)) for i in range(nf)]
            pb[:] = [P(nc.psum_tensor("pb%d_%d" % (i, uid[0]), [128, 1024], BF16)) for i in range(nb)]
            G_pf[:] = [S.buf("pf%d" % i) for i in range(nf)]
            G_pb[:] = [S.buf("pb%d" % i) for i in range(nb)]
            G_pbh[0] = S.buf("pbh")

        ident = E(sbt("ident_s", [128, 128], BF16))
        G_ident = S.buf("G_ident")
        G_mod = S.buf("G_modscr")
        G_KT = [S.buf("G_KT%d" % i) for i in range(16)]
        G_VV = [S.buf("G_VV%d" % i) for i in range(12)]
        G_QT = [S.buf("G_QT%d" % i) for i in range(16)]
        G_OT = [S.buf("G_OT%d" % i) for i in range(16)]
        G_GS = S.buf("G_GS")
        G_y = S.buf("G_y")
        S.dma("sp", ident[:], ident_d[:, :], w=[G_ident])

        phase = [0]

        def phase_done():
            S.flush(barrier=True)
            phase[0] += 1
            return phase[0] >= NPHASE

        w_ada_v = w_ada.rearrange("(kc p) n -> p kc n", p=128)

        def ada_setup(P):
            ccol = P(sbt("ccol", [128, 16], F32))
            scb = P(sbt("scb", [128, 16], BF16))
            screp = P(sbt("screp", [128, 16, 128], BF16))
            wa = [P(sbt("wa%d" % i, [128, 16, 512], BF16)) for i in range(2)]
            br = [P(sbt("br%d" % i, [128, 512], F32)) for i in range(2)]
            mrow = [P(sbt("mrow%d" % i, [128, 512], F32)) for i in range(2)]
            b_ccol, b_scb, b_screp = S.buf("ccol"), S.buf("scb"), S.buf("screp")
            b_wa, b_br, b_mrow = S.bufs_n("wa", 2), S.bufs_n("br", 2), S.bufs_n("mrow", 2)
            S.dma("sp", ccol[:], c_col[:, :], w=[b_ccol])
            S.op("act", lambda e: e.activation(out=scb[:], in_=ccol[:], func=AF.Silu), r=[b_ccol], w=[b_scb])
            S.op("dve", lambda e: e.tensor_copy(screp[:], scb[:].unsqueeze(2).to_broadcast([128, 16, 128])),
                 r=[b_scb], w=[b_screp])

            def load(cg):
                sl = cg % 2
                S.dma("pool", wa[sl][:], w_ada_v[:, :, cg * 512:(cg + 1) * 512], w=[b_wa[sl]])
                S.dma("sp", br[sl][:], b_ada[0:1, cg * 512:(cg + 1) * 512].partition_broadcast(128), w=[b_br[sl]])

            def compute(cg, bank):
                sl = cg % 2
                pt = pf[bank]
                gb = G_pf[bank]
                for kc in range(16):
                    S.op("pe", lambda e, kc=kc: e.matmul(pt[:], lhsT=screp[:, kc, :], rhs=wa[sl][:, kc, :],
                                                         start=(kc == 0), stop=(kc == 15)),
                         r=[b_screp, b_wa[sl]], w=[gb])
                addc = 1.0 if (cg // 4) in (1, 4) else 0.0
                S.op("dve", lambda e: e.scalar_tensor_tensor(out=mrow[sl][:], in0=pt[:], scalar=addc, in1=br[sl][:],
                                                             op0=ALU.add, op1=ALU.add),
                     r=[gb, b_br[sl]], w=[b_mrow[sl]])
                S.dma("sp", modscr[:, cg * 512:(cg + 1) * 512], mrow[sl][:], r=[b_mrow[sl]], w=[G_mod])

            return load, compute

        with ExitStack() as ps:
            P = ps.enter_context
            psum_alloc(P, 6, 2)
            ada_load, ada_compute = ada_setup(P)
            ada_load(0)
            for cg in range(8):
                if cg + 1 < 8:
                    ada_load(cg + 1)
                ada_compute(cg, cg % 2)
            stop = phase_done()

        def KG(ap3, i=None):
            return ap3

        def run_proj_all(supers):
            ntile = 8
            with ExitStack() as ps:
                P = ps.enter_context
                psum_alloc(P, 6, 2)
                m1 = P(sbt("m1", [128, D], F32))
                m2 = P(sbt("m2", [128, D], F32))
                cs = P(sbt("cs", [128, NT + NO, 32], F32))
                gq = P(sbt("gq", [128, 128], F32))
                gk = P(sbt("gk", [128, 3, 128], F32))
                gmq = P(sbt("gmq", [128, 128], F32))
                gmk = P(sbt("gmk", [128, 128], F32))
                xt = [P(sbt("xt%d" % i, [128, D], F32)) for i in range(2)]
                junk = P(sbt("junk", [128, D], BF16))
                t1 = P(sbt("t1", [128, D], F32))
                hb = P(sbt("hb", [128, D], BF16))
                ss = P(sbt("ss", [128, 4], F32))
                hT = P(sbt("hT", [128, 16, ntile * 128], BF16))
                wb = [P(sbt("wb%d" % i, [128, 16, 512], BF16)) for i in range(2)]
                sqL = [P(sbt("sq", [128, 512], BF16)) for _ in range(3)]
                knL = [P(sbt("kn", [128, 4, 128], F32)) for _ in range(3)]
                kbL = [P(sbt("kb", [128, 4, 128], BF16)) for _ in range(3)]
                hsL = [P(sbt("hs", [128, 8], F32)) for _ in range(3)]
                rpL = [P(sbt("rp", [128, 4, 4, 16], F32)) for _ in range(3)]
                stT = [P(sbt("stT%d" % i, [128, 4, ntile * 128], BF16)) for i in range(2)]
                stV = [P(sbt("stV%d" % i, [128, ntile, 4, 128], BF16)) for i in range(2)]
                gsg = P(sbt("gsg", [128, NO, 24], F32))
                b_m, b_cs, b_g = S.buf("m"), S.buf("cs"), S.buf("g")
                b_xt = S.bufs_n("xt", 2)
                b_junk, b_t1, b_hb, b_ss = S.buf("junk"), S.buf("t1"), S.buf("hb"), S.buf("ss")
                b_hTt = S.bufs_n("hTt", ntile)
                b_wb = S.bufs_n("wb", 2)
                b_sqL, b_knL, b_kbL, b_hsL, b_rpL = (S.bufs_n("sq", 3), S.bufs_n("kn", 3), S.bufs_n("kb", 3),
                                                     S.bufs_n("hs", 3), S.bufs_n("rp", 3))
                b_stT, b_stV = S.bufs_n("stT", 2), S.bufs_n("stV", 2)
                b_gsg = S.buf("gsg")
                S.dma("sp", m1[:], modscr[:, D:2 * D], r=[G_mod], w=[b_m])
                S.dma("sp", m2[:], modscr[:, 0:D], r=[G_mod], w=[b_m])
                S.dma("sp", cs[:, 0:NT, :], cs_full[:, :, :], w=[b_cs])
                S.dma("sp", cs[:, NT:NT + NO, :], cs_own[:, :, :], w=[b_cs])
                S.dma("sp", gq[:], g_nq[:, :], w=[b_g])
                S.dma("sp", gk[:], g_nk[:, :, :], w=[b_g])
                S.dma("sp", gmq[:], g_mq[:, :], w=[b_g])
                S.dma("sp", gmk[:], g_mk[:, :], w=[b_g])
                gains = {"nq": gq[:], "k1": gk[:, 1, :], "k2": gk[:, 2, :], "mq": gmq[:], "mk": gmk[:]}
                w_in_v = w_in.rearrange("(kc p) n -> p kc n", p=128)
                NS = len(supers)

                def xsrc_tile(gt):
                    return supers[gt // ntile][0][(gt % ntile) * 128:(gt % ntile + 1) * 128, :]

                S.dma("sp", xt[0][:], xsrc_tile(0), w=[b_xt[0]])

                def hTgen(gt):
                    tt = gt % ntile
                    sl = gt % 2
                    if gt + 1 < NS * ntile:
                        S.dma("sp", xt[1 - sl][:], xsrc_tile(gt + 1), w=[b_xt[1 - sl]])
                    S.op("act", lambda e: e.activation(out=junk[:], in_=xt[sl][:], func=AF.Square, accum_out=ss[:, 0:1]),
                         r=[b_xt[sl]], w=[b_junk, b_ss])
                    S.op("act", lambda e: e.activation(out=ss[:, 1:2], in_=ss[:, 0:1], func=AF.Sqrt, scale=1.0 / D, bias=1e-6),
                         r=[b_ss], w=[b_ss])
                    S.op("dve", lambda e: e.reciprocal(ss[:, 2:3], ss[:, 1:2]), r=[b_ss], w=[b_ss])
                    S.op("dve", lambda e: e.scalar_tensor_tensor(out=t1[:], in0=xt[sl][:], scalar=ss[:, 2:3], in1=m1[:],
                                                                 op0=ALU.mult, op1=ALU.mult),
                         r=[b_xt[sl], b_ss, b_m], w=[b_t1])
                    S.op("dve", lambda e: e.tensor_tensor(out=hb[:], in0=t1[:], in1=m2[:], op=ALU.add),
                         r=[b_t1, b_m], w=[b_hb])
                    for half in range(2):
                        for q_ in range(8):
                            kc = half * 8 + q_
                            S.op("pe", lambda e, half=half, q_=q_, kc=kc: e.transpose(
                                pb[half][:, q_ * 128:(q_ + 1) * 128], hb[:, kc * 128:(kc + 1) * 128], ident[:]),
                                r=[b_hb, G_ident], w=[G_pb[half]])
                        if half == 0:
                            S.op("act", lambda e, half=half: e.activation(
                                out=hT[:, half * 8:(half + 1) * 8, tt * 128:(tt + 1) * 128],
                                in_=pb[half][:].rearrange("p (a b) -> p a b", a=8), func=AF.Copy),
                                r=[G_pb[half]], w=[b_hTt[tt]])
                        else:
                            S.op("dve", lambda e, half=half: e.tensor_copy(
                                hT[:, half * 8:(half + 1) * 8, tt * 128:(tt + 1) * 128],
                                pb[half][:].rearrange("p (a b) -> p a b", a=8)),
                                r=[G_pb[half]], w=[b_hTt[tt]])

                steps = []
                for si, (xs_, cs0, groups) in enumerate(supers):
                    for gi in range(len(groups)):
                        for tt in range(ntile):
                            steps.append((si, gi, tt))
                gseq = [(si, gi) for si, (xs_, cs0, groups) in enumerate(supers) for gi in range(len(groups))]
                gidx = {sg: k for k, sg in enumerate(gseq)}

                def ginfo(si, gi):
                    c0, ncol, heads = supers[si][2][gi]
                    return c0, ncol, heads, gidx[(si, gi)] % 2

                c0_, nc_, _, _ = ginfo(0, 0)
                S.dma("pool", wb[0][:, :, 0:nc_], w_in_v[:, :, c0_:c0_ + nc_], w=[b_wb[0]])

                def stage_A(n):
                    si, gi, tt = steps[n]
                    c0, ncol, heads, sl = ginfo(si, gi)
                    if tt == 0:
                        k = gidx[(si, gi)]
                        if k + 1 < len(gseq):
                            c0n, ncn, _, sln = ginfo(*gseq[k + 1])
                            S.dma("pool", wb[sln][:, :, 0:ncn], w_in_v[:, :, c0n:c0n + ncn], w=[b_wb[sln]])
                    bk = 2 + (n % 4)
                    pt = pf[bk]
                    for kc in range(16):
                        S.op("pe", lambda e, kc=kc: e.matmul(
                            pt[:, 0:ncol], lhsT=hT[:, kc, tt * 128:(tt + 1) * 128], rhs=wb[sl][:, kc, 0:ncol],
                            start=(kc == 0), stop=(kc == 15)),
                            r=[b_hTt[tt], b_wb[sl]], w=[G_pf[bk]])
                    if gi == len(supers[si][2]) - 1 and si + 1 < NS:
                        hTgen((si + 1) * ntile + tt)

                def scr(n):
                    k3 = n % 3
                    return (sqL[k3], knL[k3], kbL[k3], hsL[k3], rpL[k3], b_sqL[k3], b_knL[k3], b_kbL[k3], b_hsL[k3], b_rpL[k3])

                def hk(heads):
                    nK = sum(1 for hh in heads if hh[0] == "K")
                    nC = sum(1 for hh in heads if hh[0] == "C")
                    return nK, nC, len(heads) - nK - nC

                def stage_B1(n):
                    si, gi, tt = steps[n]
                    c0, ncol, heads, sl = ginfo(si, gi)
                    bk = 2 + (n % 4)
                    pt = pf[bk]
                    sq, kn, kb, hs, rp, b_sq, b_kn, b_kb, b_hs, b_rp = scr(n)
                    if heads == "gates":
                        S.op("act", lambda e: e.activation(out=gsg[:, tt, :], in_=pt[:, 0:24], func=AF.Sigmoid),
                             r=[G_pf[bk]], w=[b_gsg])
                        return
                    nK, nC, nV = hk(heads)
                    for h in range(nK):
                        S.op("act", lambda e, h=h: e.activation(out=sq[:, h * 128:(h + 1) * 128], in_=pt[:, h * 128:(h + 1) * 128],
                                                                func=AF.Square, accum_out=hs[:, h:h + 1]),
                             r=[G_pf[bk]], w=[b_sq, b_hs])
                    if nK > 0:
                        S.op("act", lambda e: e.activation(out=hs[:, 4:4 + nK], in_=hs[:, 0:nK], func=AF.Sqrt, scale=1.0 / 128,
                                                           bias=1e-6), r=[b_hs], w=[b_hs])
                    if nC > 0:
                        S.op("act", lambda e: e.activation(out=kb[:, 0:nC, :], in_=pt[:, 0:nC * 128].rearrange("p (h d) -> p h d", h=nC),
                                                           func=AF.Copy), r=[G_pf[bk]], w=[b_kb])
                    if nV > 0:
                        nTr = nK + nC
                        S.op("act", lambda e: e.activation(
                            out=stV[sl][:, tt, 0:nV, :],
                            in_=pt[:, nTr * 128:(nTr + nV) * 128].rearrange("p (h d) -> p h d", h=nV), func=AF.Copy),
                            r=[G_pf[bk]], w=[b_stV[sl]])

                def stage_B2(n):
                    si, gi, tt = steps[n]
                    c0, ncol, heads, sl = ginfo(si, gi)
                    if heads == "gates":
                        return
                    nK, nC, nV = hk(heads)
                    if nK == 0:
                        return
                    bk = 2 + (n % 4)
                    pt = pf[bk]
                    sq, kn, kb, hs, rp, b_sq, b_kn, b_kb, b_hs, b_rp = scr(n)
                    gain = gains[heads[0][1]]
                    pv = pt[:, 0:nK * 128].rearrange("p (h d) -> p h d", h=nK)
                    S.op("dve", lambda e: e.reciprocal(hs[:, 0:nK], hs[:, 4:4 + nK]), r=[b_hs], w=[b_hs])
                    S.op("dve", lambda e: e.tensor_tensor(out=kn[:, 0:nK, :], in0=pv,
                                                          in1=hs[:, 0:nK].unsqueeze(2).to_broadcast([128, nK, 128]), op=ALU.mult),
                         r=[G_pf[bk], b_hs], w=[b_kn])
                    S.op("dve", lambda e: e.tensor_tensor(out=kn[:, 0:nK, :], in0=kn[:, 0:nK, :],
                                                          in1=gain.unsqueeze(1).to_broadcast([128, nK, 128]), op=ALU.mult),
                         r=[b_kn, b_g], w=[b_kn])

                def stage_B3(n):
                    si, gi, tt = steps[n]
                    c0, ncol, heads, sl = ginfo(si, gi)
                    if heads == "gates":
                        return
                    nK, nC, nV = hk(heads)
                    if nK == 0:
                        return
                    sq, kn, kb, hs, rp, b_sq, b_kn, b_kb, b_hs, b_rp = scr(n)
                    ct_ = supers[si][1] + tt
                    cosb = cs[:, ct_, 0:16].unsqueeze(1).to_broadcast([128, nK, 16])
                    sinb = cs[:, ct_, 16:32].unsqueeze(1).to_broadcast([128, nK, 16])
                    x1 = kn[:, 0:nK, 0:16]
                    x2 = kn[:, 0:nK, 16:32]
                    S.op("pool", lambda e: e.tensor_tensor(out=rp[:, 0, 0:nK, :], in0=x1, in1=cosb, op=ALU.mult), r=[b_kn, b_cs], w=[b_rp])
                    S.op("pool", lambda e: e.tensor_tensor(out=rp[:, 1, 0:nK, :], in0=x2, in1=sinb, op=ALU.mult), r=[b_kn, b_cs], w=[b_rp])
                    S.op("pool", lambda e: e.tensor_tensor(out=rp[:, 2, 0:nK, :], in0=x2, in1=cosb, op=ALU.mult), r=[b_kn, b_cs], w=[b_rp])
                    S.op("pool", lambda e: e.tensor_tensor(out=rp[:, 3, 0:nK, :], in0=x1, in1=sinb, op=ALU.mult), r=[b_kn, b_cs], w=[b_rp])
                    S.op("pool", lambda e: e.tensor_tensor(out=kb[:, 0:nK, 0:16], in0=rp[:, 0, 0:nK, :], in1=rp[:, 1, 0:nK, :],
                                                           op=ALU.subtract), r=[b_rp], w=[b_kb])
                    S.op("pool", lambda e: e.tensor_tensor(out=kb[:, 0:nK, 16:32], in0=rp[:, 2, 0:nK, :], in1=rp[:, 3, 0:nK, :],
                                                           op=ALU.add), r=[b_rp], w=[b_kb])
                    S.op("act", lambda e: e.activation(out=kb[:, 0:nK, 32:128], in_=kn[:, 0:nK, 32:128], func=AF.Copy),
                         r=[b_kn], w=[b_kb])

                def stage_B4(n):
                    si, gi, tt = steps[n]
                    c0, ncol, heads, sl = ginfo(si, gi)
                    if heads == "gates":
                        if tt == ntile - 1:
                            S.dma("sp", GS[:, :, :], gsg[:], r=[b_gsg], w=[G_GS])
                        return
                    nK, nC, nV = hk(heads)
                    nTr = nK + nC
                    sq, kn, kb, hs, rp, b_sq, b_kn, b_kb, b_hs, b_rp = scr(n)
                    if nTr > 0:
                        ph = n % 2
                        for hq_ in range(nTr):
                            S.op("pe", lambda e, hq_=hq_: e.transpose(pb[ph][:, hq_ * 128:(hq_ + 1) * 128], kb[:, hq_, :], ident[:]),
                                 r=[b_kb, G_ident], w=[G_pb[ph]])
                        S.op("dve", lambda e: e.tensor_copy(stT[sl][:, 0:nTr, tt * 128:(tt + 1) * 128],
                                                            pb[ph][:, 0:nTr * 128].rearrange("p (h t) -> p h t", h=nTr)),
                             r=[G_pb[ph]], w=[b_stT[sl]])
                    if tt == ntile - 1:
                        hT_i = 0
                        hV_i = 0
                        for (kind, gk_, dst, idx, tok0) in heads:
                            if kind in ("K", "C"):
                                dt_, gb = (KT, G_KT) if dst == "KT" else (QT, G_QT)
                                S.dma("sp", dt_[idx, :, tok0:tok0 + ntile * 128], stT[sl][:, hT_i, :], r=[b_stT[sl]], w=[gb[idx]])
                                hT_i += 1
                            else:
                                S.dma("sp", VV[idx, tok0:tok0 + ntile * 128, :].rearrange("(t p) d -> p t d", p=128),
                                      stV[sl][:, :, hV_i, :], r=[b_stV[sl]], w=[G_VV[idx]])
                                hV_i += 1

                hTgen(0)
                hTgen(1)
                NSt = len(steps)
                for n in range(NSt + 4):
                    if 0 <= n - 4 < NSt:
                        stage_B4(n - 4)
                    if 0 <= n - 3 < NSt:
                        stage_B3(n - 3)
                    if 0 <= n - 2 < NSt:
                        stage_B2(n - 2)
                    if 0 <= n - 1 < NSt:
                        stage_B1(n - 1)
                    if n < NSt:
                        si, gi, tt = steps[n]
                        if si == 0 and gi == 0 and tt + 2 < ntile:
                            hTgen(tt + 2)
                        stage_A(n)
                return phase_done()

        if not stop:
            supers = []
            for st in range(4):
                tok0 = st * 1024
                groups = [
                    (1024, 512, [("C", None, "KT", 12, tok0), ("C", None, "KT", 13, tok0),
                                 ("C", None, "KT", 14, tok0), ("C", None, "KT", 15, tok0)]),
                    (1536, 512, [("K", "k1", "KT", 0, tok0), ("K", "k1", "KT", 1, tok0),
                                 ("V", None, "VV", 0, tok0), ("V", None, "VV", 1, tok0)]),
                    (2048, 512, [("K", "k2", "KT", 2, tok0), ("K", "k2", "KT", 3, tok0),
                                 ("V", None, "VV", 2, tok0), ("V", None, "VV", 3, tok0)]),
                    (3608, 512, [("K", "mk", "KT", 4 + h, tok0) for h in range(4)]),
                    (3608 + 512, 512, [("K", "mk", "KT", 8 + h, tok0) for h in range(4)]),
                    (4632, 512, [("V", None, "VV", 4 + h, tok0) for h in range(4)]),
                    (4632 + 512, 512, [("V", None, "VV", 8 + h, tok0) for h in range(4)]),
                ]
                supers.append((x_full[tok0:tok0 + 1024, :], st * 8, groups))
            groups = [
                (0, 512, [("K", "nq", "QT", h, 0) for h in range(4)]),
                (512, 512, [("K", "nq", "QT", 4 + h, 0) for h in range(4)]),
                (2560, 24, "gates"),
                (2584, 512, [("K", "mq", "QT", 8 + h, 0) for h in range(4)]),
                (2584 + 512, 512, [("K", "mq", "QT", 12 + h, 0) for h in range(4)]),
            ]
            supers.append((x_own[:, :], NT, groups))
            stop = run_proj_all(supers)

        def emit_normrope(src, n, gain, cosb, sinb, dst, sq, hs, kn, rp, bsrc, bdst, bs):
            b_sq, b_hs, b_kn, b_rp, b_g, b_cs = bs
            S.op("act", lambda e: e.activation(out=sq[:, 0:n * 128].rearrange("p (h d) -> p h d", h=n), in_=src,
                                               func=AF.Square), r=[bsrc], w=[b_sq])
            S.op("dve", lambda e: e.tensor_reduce(out=hs[:, 0:n], in_=sq[:, 0:n * 128].rearrange("p (h d) -> p h d", h=n),
                                                  axis=AX.X, op=ALU.add), r=[b_sq], w=[b_hs])
            S.op("act", lambda e: e.activation(out=hs[:, 4:4 + n], in_=hs[:, 0:n], func=AF.Sqrt, scale=1.0 / 128,
                                               bias=1e-6), r=[b_hs], w=[b_hs])
            S.op("dve", lambda e: e.reciprocal(hs[:, 0:n], hs[:, 4:4 + n]), r=[b_hs], w=[b_hs])
            S.op("dve", lambda e: e.tensor_tensor(out=kn[:, 0:n, :], in0=src,
                                                  in1=hs[:, 0:n].unsqueeze(2).to_broadcast([128, n, 128]), op=ALU.mult),
                 r=[bsrc, b_hs], w=[b_kn])
            S.op("dve", lambda e: e.tensor_tensor(out=kn[:, 0:n, :], in0=kn[:, 0:n, :], in1=gain, op=ALU.mult),
                 r=[b_kn, b_g], w=[b_kn])
            if cosb is None:
                S.op("act", lambda e: e.activation(out=dst, in_=kn[:, 0:n, :], func=AF.Copy), r=[b_kn], w=[bdst])
                return
            x1 = kn[:, 0:n, 0:16]
            x2 = kn[:, 0:n, 16:32]
            S.op("dve", lambda e: e.tensor_tensor(out=rp[:, 0, 0:n, :], in0=x1, in1=cosb, op=ALU.mult), r=[b_kn, b_cs], w=[b_rp])
            S.op("dve", lambda e: e.tensor_tensor(out=rp[:, 1, 0:n, :], in0=x2, in1=sinb, op=ALU.mult), r=[b_kn, b_cs], w=[b_rp])
            S.op("dve", lambda e: e.tensor_tensor(out=rp[:, 2, 0:n, :], in0=x2, in1=cosb, op=ALU.mult), r=[b_kn, b_cs], w=[b_rp])
            S.op("dve", lambda e: e.tensor_tensor(out=rp[:, 3, 0:n, :], in0=x1, in1=sinb, op=ALU.mult), r=[b_kn, b_cs], w=[b_rp])
            S.op("dve", lambda e: e.tensor_tensor(out=dst[:, :, 0:16], in0=rp[:, 0, 0:n, :], in1=rp[:, 1, 0:n, :],
                                                  op=ALU.subtract), r=[b_rp], w=[bdst])
            S.op("dve", lambda e: e.tensor_tensor(out=dst[:, :, 16:32], in0=rp[:, 2, 0:n, :], in1=rp[:, 3, 0:n, :],
                                                  op=ALU.add), r=[b_rp], w=[bdst])
            S.op("act", lambda e: e.activation(out=dst[:, :, 32:128], in_=kn[:, 0:n, 32:128], func=AF.Copy),
                 r=[b_kn], w=[bdst])

        kcT = E(sbt("kcT", [128, 2, 256], BF16))
        vca = E(sbt("vca", [128, 2, 2, 193], BF16))
        G_kcT, G_vca = S.buf("G_kcT"), S.buf("G_vca")
        if not stop:
            with ExitStack() as ps:
                P = ps.enter_context
                psum_alloc(P, 6, 2)
                w1s = [P(sbt("w1s%d" % i, [128, 32, 128], BF16)) for i in range(2)]
                w2s = [P(sbt("w2s%d" % i, [128, 128], BF16)) for i in range(2)]
                pe32 = [P(sbt("pe32%d" % i, [32, 128], F32)) for i in range(2)]
                id32 = P(sbt("id32", [32, 32], F32))
                peT = [P(sbt("peT%d" % i, [128, 32], F32)) for i in range(2)]
                kcm = P(sbt("kcm", [128, 256, 16], BF16))
                kp = P(sbt("kp", [128, 32, 255], BF16))
                hid = P(sbt("hid", [128, 256], BF16))
                cb = P(sbt("cb", [128, 2, 128], BF16))
                csC = P(sbt("csC", [128, 2, 32], F32))
                gk0 = P(sbt("gk0", [128, 128], F32))
                sq = P(sbt("sq2", [128, 512], F32))
                hs = P(sbt("hs2", [128, 8], F32))
                kn = P(sbt("kn2", [128, 4, 128], F32))
                rp = P(sbt("rp2", [128, 4, 4, 16], F32))
                b_w, b_pe, b_peT, b_kcm, b_kp, b_hid, b_cb = (S.buf("w"), S.buf("pe"), S.buf("peT"), S.buf("kcm"),
                                                               S.buf("kp"), S.buf("hid"), S.buf("cb"))
                bs = (S.buf("sq"), S.buf("hs"), S.buf("kn"), S.buf("rp"), S.buf("g"), S.buf("cs"))
                S.op("pool", lambda e: e.memset(hid[:], 0.0), w=[b_hid])
                for g in range(2):
                    S.dma("sp", vca[:, g, :, 128:193], ovaug_d[:, :, :], w=[G_vca])
                S.dma("sp", csC[:], cs_cmp[:, :, :], w=[bs[5]])
                S.dma("sp", gk0[:], g_nk[:, 0, :], w=[bs[4]])
                S.dma("sp", id32[:], ident32_d[:, :], w=[b_pe])
                for kv, (w1d, w2d, ped) in enumerate(((w1_k, w2_k, pe_k), (w1_v, w2_v, pe_v))):
                    S.dma("pool", w1s[kv][:], w1d.rearrange("(p d) o -> d p o", d=128), w=[b_w])
                    S.dma("pool", w2s[kv][:], w2d[:, :], w=[b_w])
                    S.dma("sp", pe32[kv][:], ped[:, :], w=[b_pe])
                    S.op("pe", lambda e, kv=kv: e.matmul(pf[0][:, 0:32], lhsT=pe32[kv][:], rhs=id32[:], start=True, stop=True),
                         r=[b_pe], w=[G_pf[0]])
                    S.op("dve", lambda e, kv=kv: e.tensor_copy(peT[kv][:], pf[0][:, 0:32]), r=[G_pf[0]], w=[b_peT])
                for kv in range(2):
                    for g in range(2):
                        S.dma("sp", kcm[:].rearrange("p a b -> p (a b)"), KT[12 + 2 * kv + g, :, :], r=[G_KT[12 + 2 * kv + g]],
                              w=[b_kcm])
                        S.op("dve", lambda e, kv=kv: e.tensor_tensor(
                            out=kp[:, 0:16, :], in0=kcm[:, 0:255, :].rearrange("d c p -> d p c"),
                            in1=peT[kv][:, 0:16].unsqueeze(2).to_broadcast([128, 16, 255]), op=ALU.add),
                            r=[b_kcm, b_peT], w=[b_kp])
                        S.op("dve", lambda e, kv=kv: e.tensor_tensor(
                            out=kp[:, 16:32, :], in0=kcm[:, 1:256, :].rearrange("d c p -> d p c"),
                            in1=peT[kv][:, 16:32].unsqueeze(2).to_broadcast([128, 16, 255]), op=ALU.add),
                            r=[b_kcm, b_peT], w=[b_kp])
                        for p in range(32):
                            S.op("pe", lambda e, kv=kv, p=p: e.matmul(pf[1][:, 0:255], lhsT=w1s[kv][:, p, :], rhs=kp[:, p, :],
                                                                      start=(p == 0), stop=(p == 31)),
                                 r=[b_w, b_kp], w=[G_pf[1]])
                        S.op("act", lambda e: e.activation(out=hid[:, 0:255], in_=pf[1][:, 0:255], func=AF.Silu),
                             r=[G_pf[1]], w=[b_hid])
                        for ct in range(2):
                            S.op("pe", lambda e, kv=kv, ct=ct: e.matmul(pf[0][:, ct * 128:(ct + 1) * 128],
                                                                        lhsT=hid[:, ct * 128:(ct + 1) * 128], rhs=w2s[kv][:],
                                                                        start=True, stop=True),
                                 r=[b_w, b_hid], w=[G_pf[0]])
                        src = pf[0][:, 0:256].rearrange("p (h d) -> p h d", h=2)
                        if kv == 1:
                            S.op("act", lambda e, g=g, src=src: e.activation(out=vca[:, g, :, 0:128], in_=src, func=AF.Copy),
                                 r=[G_pf[0]], w=[G_vca])
                        else:
                            emit_normrope(src, 2, gk0[:].unsqueeze(1).to_broadcast([128, 2, 128]), csC[:, :, 0:16],
                                          csC[:, :, 16:32], cb[:], sq, hs, kn, rp, G_pf[0], b_cb, bs)
                            for ct in range(2):
                                S.op("pe", lambda e, ct=ct: e.transpose(pb[0][:, ct * 128:(ct + 1) * 128], cb[:, ct, :], ident[:]),
                                     r=[b_cb, G_ident], w=[G_pb[0]])
                            S.op("dve", lambda e, g=g: e.tensor_copy(kcT[:, g, :], pb[0][:, 0:256]), r=[G_pb[0]], w=[G_kcT])
                stop = phase_done()

        def Ov(t, w):
            return t[:, 0:2 * w].rearrange("p (a b) -> p a b", a=2)

        def head_out(src4, gsl, ob, sq, hs, kn, bsrc, b_ob, bs, ot_idx0, i, stg, b_stg):
            b_sq, b_hs, b_kn, b_rp, b_g, b_cs = bs
            sqv = sq[:, 0:512].rearrange("p (h d) -> p h d", h=4)
            S.op("dve", lambda e: e.tensor_tensor(out=sqv, in0=src4, in1=src4, op=ALU.mult), r=[bsrc], w=[b_sq])
            S.op("dve", lambda e: e.tensor_reduce(out=hs[:, 0:4], in_=sqv, axis=AX.X, op=ALU.add), r=[b_sq], w=[b_hs])
            S.op("act", lambda e: e.activation(out=hs[:, 4:8], in_=hs[:, 0:4], func=AF.Ln, scale=1.0 / 128, bias=1e-6),
                 r=[b_hs], w=[b_hs])
            S.op("act", lambda e: e.activation(out=hs[:, 0:4], in_=hs[:, 4:8], func=AF.Exp, scale=-0.5), r=[b_hs], w=[b_hs])
            S.op("dve", lambda e: e.tensor_tensor(out=kn[:, 0:4, :], in0=src4,
                                                  in1=hs[:, 0:4].unsqueeze(2).to_broadcast([128, 4, 128]), op=ALU.mult),
                 r=[bsrc, b_hs], w=[b_kn])
            S.op("dve", lambda e: e.tensor_tensor(out=ob[:], in0=kn[:, 0:4, :], in1=gsl, op=ALU.mult),
                 r=[b_kn, b_g], w=[b_ob])
            for r in range(4):
                S.op("pe", lambda e, r=r: e.transpose(pb[0][:, 512 + r * 128:512 + (r + 1) * 128], ob[:, r, :], ident[:]),
                     r=[b_ob, G_ident], w=[G_pbh[0]])
            S.op("dve", lambda e: e.tensor_copy(stg[:], pb[0][:, 512:1024].rearrange("p (h t) -> p h t", h=4)),
                 r=[G_pbh[0]], w=[b_stg])
            S.dma("sp", OT[ot_idx0:ot_idx0 + 4, :, i * 128:(i + 1) * 128].rearrange("h d t -> d h t"), stg[:],
                  r=[b_stg], w=[G_OT[ot_idx0 + k] for k in range(4)])

        if not stop:
            with ExitStack() as ps:
                P = ps.enter_context
                psum_alloc(P, 7, 1)
                ksT = P(sbt("ksT", [128, T], BF16))
                kwT = P(sbt("kwT", [128, T], BF16))
                vs = P(sbt("vs", [128, NT, 129], BF16))
                vw = P(sbt("vw", [128, NT, 129], BF16))
                qT = [P(sbt("qT%d" % k, [128, 4, 128], BF16)) for k in range(2)]
                cmpm = P(sbt("cmpm", [128, NO, 2, 128], BF16))
                cmk = P(sbt("cmk", [128, 4, 128], BF16))
                wmk = P(sbt("wmk", [128, 8, 128], BF16))
                adds = P(sbt("adds", [128, NO, 64], F32))
                xsel = P(sbt("xsel", [64, T], BF16))
                gs = P(sbt("gs", [128, NO, 24], F32))
                gout = P(sbt("gout", [128, 1024], F32))
                Ec = P(sbt("Ec", [128, 2, 512], BF16))
                Et = [P(sbt("Et%d" % k, [128, 512], BF16)) for k in range(3)]
                Oc = P(sbt("Oc", [128, 4, 193], F32))
                Os = P(sbt("Os", [128, 4, 129], F32))
                Ow = P(sbt("Ow", [128, 4, 129], F32))
                rinv = P(sbt("rinv", [128, 4, 3], F32))
                coef = P(sbt("coef", [128, 4, 3], F32))
                imp = P(sbt("imp", [128, 64], F32))
                sc2 = P(sbt("sc2s", [128, 64], F32))
                m8 = P(sbt("m8", [128, 16], F32))
                negb = P(sbt("negb", [128, 64], BF16))
                negT = P(sbt("negT", [64, 128], BF16))
                o4 = P(sbt("o4", [128, 4, 128], F32))
                ob = P(sbt("ob", [128, 4, 128], BF16))
                stg = P(sbt("stg", [128, 4, 128], BF16))
                sq = P(sbt("sq3", [128, 512], F32))
                hs = P(sbt("hs3", [128, 8], F32))
                kn = P(sbt("kn3", [128, 4, 128], F32))
                b_kv, b_q, b_c = S.buf("kv"), S.bufs_n("q", 2), S.buf("consts")
                b_Ec, b_Et, b_Oc, b_Os, b_Ow = S.buf("Ec"), S.bufs_n("Et", 3), S.buf("Oc"), S.buf("Os"), S.buf("Ow")
                b_ri, b_co, b_imp, b_sc2, b_m8, b_negb, b_negT = (S.buf("ri"), S.buf("co"), S.buf("imp"), S.buf("sc2"),
                                                                   S.buf("m8"), S.buf("negb"), S.buf("negT"))
                b_o4, b_ob, b_stg = S.buf("o4"), S.buf("ob"), S.buf("stg")
                bs = (S.buf("sq"), S.buf("hs"), S.buf("kn"), S.buf("rp"), b_c, b_c)
                S.dma("sp", cmpm[:], cmpmask_d[:, :, :, :], w=[b_c])
                S.dma("sp", cmk[:], cmask_d[:, :, :], w=[b_c])
                S.dma("sp", wmk[:], wmask_d[:, :, :], w=[b_c])
                S.dma("sp", adds[:], addsel_d[:, :, :], w=[b_c])
                S.dma("sp", xsel[:], xsel_d[:, :], w=[b_c])
                S.dma("sp", gs[:], GS[:, :, :], r=[G_GS], w=[b_c])
                S.dma("sp", gout[:], g_out[:, 0:1024], w=[b_c])
                S.op("pool", lambda e: e.memset(vs[:, :, 128:129], 1.0), w=[b_kv])
                S.op("pool", lambda e: e.memset(vw[:, :, 128:129], 1.0), w=[b_kv])
                qn = 0
                for g in range(2):
                    S.dma("sp", ksT[:], KT[g, :, :], r=[G_KT[g]], w=[b_kv])
                    S.dma("sp", kwT[:], KT[2 + g, :, :], r=[G_KT[2 + g]], w=[b_kv])
                    S.dma("sp", vs[:, :, 0:128], VV[g, :, :].rearrange("(t p) d -> p t d", p=128), r=[G_VV[g]], w=[b_kv])
                    S.dma("sp", vw[:, :, 0:128], VV[2 + g, :, :].rearrange("(t p) d -> p t d", p=128), r=[G_VV[2 + g]],
                          w=[b_kv])
                    def q_load(i_):
                        S.dma("sp", qT[i_ % 2][:], QT[4 * g:4 * g + 4, :, i_ * 128:(i_ + 1) * 128].rearrange("h d t -> d h t"),
                              r=[G_QT[4 * g + k] for k in range(4)], w=[b_q[i_ % 2]])

                    q_load(0)
                    for i in range(NO):
                        q = qT[i % 2]
                        bq = b_q[i % 2]
                        if i + 1 < NO:
                            q_load(i + 1)
                        qv = q[:].rearrange("p h t -> p (h t)")
                        for ct in range(2):
                            S.op("pe", lambda e, ct=ct, g=g, qv=qv: e.matmul(pf[ct][:], lhsT=kcT[:, g, ct * 128:(ct + 1) * 128],
                                                                            rhs=qv, start=True, stop=False),
                                 r=[G_kcT, bq], w=[G_pf[ct]])
                            S.op("pe", lambda e, ct=ct, i=i: e.matmul(
                                pf[ct][:], lhsT=ident[:], rhs=cmpm[:, i, ct, :].unsqueeze(1).to_broadcast([128, 4, 128]),
                                start=False, stop=True), r=[G_ident, b_c], w=[G_pf[ct]])
                            S.op("act", lambda e, ct=ct: e.activation(out=Ec[:, ct, :], in_=pf[ct][:], func=AF.Exp, scale=SCALE),
                                 r=[G_pf[ct]], w=[b_Ec])
                        for r in range(4):
                            for ct in range(2):
                                S.op("pe", lambda e, r=r, ct=ct, g=g: e.matmul(
                                    Ov(pf[5 + r // 2], 193)[:, r % 2, :], lhsT=Ec[:, ct, r * 128:(r + 1) * 128],
                                    rhs=vca[:, g, ct, :], start=(ct == 0), stop=(ct == 1)),
                                    r=[b_Ec, G_vca], w=[G_pf[5 + r // 2]])
                        for hh in range(2):
                            S.op("dve", lambda e, hh=hh: e.tensor_copy(Oc[:, 2 * hh:2 * hh + 2, :], Ov(pf[5 + hh], 193)),
                                 r=[G_pf[5 + hh]], w=[b_Oc])
                        S.op("dve", lambda e: e.tensor_scalar(rinv[:, :, 0], Oc[:, :, 128], 1e-30, None, op0=ALU.add),
                             r=[b_Oc], w=[b_ri])
                        S.op("dve", lambda e: e.reciprocal(rinv[:, :, 0], rinv[:, :, 0]), r=[b_ri], w=[b_ri])
                        S.op("dve", lambda e: e.tensor_scalar(imp[:], Oc[:, 0, 129:193], rinv[:, 0, 0:1], None, op0=ALU.mult),
                             r=[b_Oc, b_ri], w=[b_imp])
                        for r in range(1, 4):
                            S.op("dve", lambda e, r=r: e.scalar_tensor_tensor(out=imp[:], in0=Oc[:, r, 129:193],
                                                                              scalar=rinv[:, r, 0:1], in1=imp[:],
                                                                              op0=ALU.mult, op1=ALU.add),
                                 r=[b_Oc, b_ri, b_imp], w=[b_imp])
                        S.op("dve", lambda e, i=i: e.tensor_tensor(out=imp[:], in0=imp[:], in1=adds[:, i, :], op=ALU.add),
                             r=[b_imp, b_c], w=[b_imp])
                        S.op("dve", lambda e: e.max(out=m8[:, 0:8], in_=imp[:]), r=[b_imp], w=[b_m8])
                        S.op("dve", lambda e: e.match_replace(out=sc2[:], in_to_replace=m8[:, 0:8], in_values=imp[:],
                                                              imm_value=-1e30), r=[b_imp, b_m8], w=[b_sc2])
                        S.op("dve", lambda e: e.max(out=m8[:, 8:16], in_=sc2[:]), r=[b_sc2], w=[b_m8])
                        S.op("dve", lambda e: e.tensor_scalar(sc2[:], imp[:], m8[:, 15:16], None, op0=ALU.is_ge),
                             r=[b_imp, b_m8], w=[b_sc2])
                        S.op("dve", lambda e: e.tensor_scalar(negb[:], sc2[:], -1.0, -NEG, op0=ALU.add, op1=ALU.mult),
                             r=[b_sc2], w=[b_negb])
                        stp = []
                        kts1 = [4 * i - 4 + w for w in range(8) if 4 * i - 4 + w >= 0]
                        for n_, kt in enumerate(kts1):
                            stp.append((1, kt, n_, n_ == len(kts1) - 1))
                        kts0 = list(range(4 * i + 4))
                        for n_, kt in enumerate(kts0):
                            stp.append((0, kt, n_, n_ == len(kts0) - 1))
                        first_slc = len(kts1)

                        def st_S(n):
                            br_, kt, n_, last = stp[n]
                            sl = n % 3
                            pt = pf[sl]
                            ii, qq = i, qv
                            if n == first_slc:
                                S.op("pe", lambda e: e.transpose(pb[0][0:64, 0:128], negb[:], ident[:]), r=[b_negb, G_ident],
                                     w=[G_pb[0]])
                                S.op("dve", lambda e: e.tensor_copy(negT[:], pb[0][0:64, 0:128]), r=[G_pb[0]], w=[b_negT])
                            Ksrc = ksT if br_ == 0 else kwT
                            S.op("pe", lambda e: e.matmul(pt[:], lhsT=Ksrc[:, kt * 128:(kt + 1) * 128], rhs=qq, start=True, stop=False),
                                 r=[b_kv, bq], w=[G_pf[sl]])
                            if br_ == 0:
                                S.op("pe", lambda e: e.matmul(
                                    pt[:], lhsT=xsel[:, kt * 128:(kt + 1) * 128],
                                    rhs=negT[:].unsqueeze(1).to_broadcast([64, 4, 128]), start=False, stop=(kt < 4 * ii)),
                                    r=[b_c, b_negT], w=[G_pf[sl]])
                                if kt >= 4 * ii:
                                    S.op("pe", lambda e: e.matmul(
                                        pt[:], lhsT=ident[:], rhs=cmk[:, kt - 4 * ii, :].unsqueeze(1).to_broadcast([128, 4, 128]),
                                        start=False, stop=True), r=[G_ident, b_c], w=[G_pf[sl]])
                            else:
                                w_ = kt - (4 * ii - 4)
                                S.op("pe", lambda e: e.matmul(
                                    pt[:], lhsT=ident[:], rhs=wmk[:, w_, :].unsqueeze(1).to_broadcast([128, 4, 128]),
                                    start=False, stop=True), r=[G_ident, b_c], w=[G_pf[sl]])
                            S.op("act", lambda e: e.activation(out=Et[sl][:], in_=pt[:], func=AF.Exp, scale=SCALE),
                                 r=[G_pf[sl]], w=[b_Et[sl]])

                        def st_PV(n):
                            br_, kt, n_, last = stp[n]
                            sl = n % 3
                            Vsrc = vs if br_ == 0 else vw
                            ob_ = 5 if br_ == 0 else 3
                            for r in range(4):
                                S.op("pe", lambda e, r=r: e.matmul(
                                    Ov(pf[ob_ + r // 2], 129)[:, r % 2, :], lhsT=Et[sl][:, r * 128:(r + 1) * 128],
                                    rhs=Vsrc[:, kt, :], start=(n_ == 0 and r % 2 == 0), stop=(last and r % 2 == 1)),
                                    r=[b_Et[sl], b_kv], w=[G_pf[ob_ + r // 2]])
                            if last:
                                Od, b_Od = (Os, b_Os) if br_ == 0 else (Ow, b_Ow)
                                for hh in range(2):
                                    S.op("dve", lambda e, hh=hh: e.tensor_copy(Od[:, 2 * hh:2 * hh + 2, :], Ov(pf[ob_ + hh], 129)),
                                         r=[G_pf[ob_ + hh]], w=[b_Od])
                                S.op("dve", lambda e: e.tensor_scalar(rinv[:, :, 1 + br_], Od[:, :, 128], 1e-30, None, op0=ALU.add),
                                     r=[b_Od], w=[b_ri])
                                S.op("dve", lambda e: e.reciprocal(rinv[:, :, 1 + br_], rinv[:, :, 1 + br_]), r=[b_ri], w=[b_ri])

                        st_S(0)
                        if len(stp) > 1:
                            st_S(1)
                        for n in range(len(stp)):
                            if n + 2 < len(stp):
                                st_S(n + 2)
                            st_PV(n)
                        S.op("dve", lambda e, g=g, i=i: e.tensor_tensor(
                            out=coef[:], in0=rinv[:], in1=gs[:, i, g * 12:(g + 1) * 12].rearrange("p (r x) -> p r x", r=4),
                            op=ALU.mult), r=[b_ri, b_c], w=[b_co])
                        for r in range(4):
                            S.op("dve", lambda e, r=r: e.tensor_scalar(o4[:, r, :], Oc[:, r, 0:128], coef[:, r, 0:1], None,
                                                                       op0=ALU.mult), r=[b_Oc, b_co], w=[b_o4])
                            S.op("dve", lambda e, r=r: e.scalar_tensor_tensor(out=o4[:, r, :], in0=Os[:, r, 0:128],
                                                                              scalar=coef[:, r, 1:2], in1=o4[:, r, :],
                                                                              op0=ALU.mult, op1=ALU.add),
                                 r=[b_Os, b_co, b_o4], w=[b_o4])
                            S.op("dve", lambda e, r=r: e.scalar_tensor_tensor(out=o4[:, r, :], in0=Ow[:, r, 0:128],
                                                                              scalar=coef[:, r, 2:3], in1=o4[:, r, :],
                                                                              op0=ALU.mult, op1=ALU.add),
                                 r=[b_Ow, b_co, b_o4], w=[b_o4])
                        head_out(o4[:], gout[:, g * 512:(g + 1) * 512].rearrange("p (h d) -> p h d", h=4), ob, sq, hs, kn,
                                 b_o4, b_ob, bs, 4 * g, i, stg, b_stg)
                stop = phase_done()

        for hq in range(2):
            if stop:
                break
            with ExitStack() as ps:
                P = ps.enter_context
                psum_alloc(P, 7, 1)
                kT4 = P(sbt("kT4", [128, 4, T], BF16))
                v4 = P(sbt("v4", [128, 4, NT, 129], BF16))
                qT = [P(sbt("mqT%d" % k, [128, 4, 128], BF16)) for k in range(2)]
                cmk = P(sbt("mcmk", [128, 4, 128], BF16))
                addm = P(sbt("addm", [128, NO, 16], F32))
                xmo = P(sbt("xmo", [16, T], BF16))
                gout = P(sbt("mgout", [128, 512], F32))
                km = P(sbt("km", [128, 64], F32))
                kmb = P(sbt("kmb", [128, 4, 16], BF16))
                Et = [P(sbt("mEt%d" % k, [128, 512], BF16)) for k in range(3)]
                Om = P(sbt("Om", [128, 4, 129], F32))
                rinv = P(sbt("mrinv", [128, 4], F32))
                scm = P(sbt("scm", [128, 4, 16], F32))
                selm = P(sbt("selm", [128, 4, 16], F32))
                m8 = P(sbt("mm8", [128, 4, 8], F32))
                negb = P(sbt("mnegb", [128, 4, 16], BF16))
                negT = P(sbt("mnegT", [16, 512], BF16))
                o4 = P(sbt("mo4", [128, 4, 128], F32))
                ob = P(sbt("mob", [128, 4, 128], BF16))
                stg = P(sbt("mstg", [128, 4, 128], BF16))
                sq = P(sbt("sq4", [128, 512], F32))
                hs = P(sbt("hs4", [128, 8], F32))
                kn = P(sbt("kn4", [128, 4, 128], F32))
                b_kv, b_q, b_c = S.buf("kv"), S.bufs_n("q", 2), S.buf("consts")
                b_km, b_Et, b_Om, b_ri, b_scm, b_selm, b_m8, b_negb, b_negT = (
                    S.buf("km"), S.bufs_n("Et", 3), S.buf("Om"), S.buf("ri"), S.buf("scm"), S.buf("selm"), S.buf("m8"),
                    S.buf("negb"), S.buf("negT"))
                b_o4, b_ob, b_stg = S.buf("o4"), S.buf("ob"), S.buf("stg")
                bs = (S.buf("sq"), S.buf("hs"), S.buf("kn"), S.buf("rp"), b_c, b_c)
                S.dma("sp", cmk[:], cmask_d[:, :, :], w=[b_c])
                S.dma("sp", addm[:], addmoba_d[:, :, :], w=[b_c])
                S.dma("sp", xmo[:], xmoba_d[:, :], w=[b_c])
                S.dma("sp", gout[:], g_out[:, 1024 + hq * 512:1024 + (hq + 1) * 512], w=[b_c])
                S.op("pool", lambda e: e.memset(v4[:, :, :, 128:129], 1.0), w=[b_kv])
                for h in range(4):
                    S.dma("sp", kT4[:, h, :], KT[4 + 4 * hq + h, :, :], r=[G_KT[4 + 4 * hq + h]], w=[b_kv])
                    S.dma("sp", v4[:, h, :, 0:128], VV[4 + 4 * hq + h, :, :].rearrange("(t p) d -> p t d", p=128),
                          r=[G_VV[4 + 4 * hq + h]], w=[b_kv])
                S.op("dve", lambda e: e.tensor_reduce(out=km[:], in_=kT4[:].rearrange("p h (n k) -> p (h n) k", k=256),
                                                      axis=AX.X, op=ALU.add), r=[b_kv], w=[b_km])
                S.op("dve", lambda e: e.tensor_scalar(kmb[:].rearrange("p h n -> p (h n)"), km[:], 1.0 / 256, None, op0=ALU.mult),
                     r=[b_km], w=[b_km])
                def mq_load(i_):
                    S.dma("sp", qT[i_ % 2][:], QT[8 + 4 * hq:12 + 4 * hq, :, i_ * 128:(i_ + 1) * 128].rearrange("h d t -> d h t"),
                          r=[G_QT[8 + 4 * hq + k] for k in range(4)], w=[b_q[i_ % 2]])

                mq_load(0)
                ada_load, ada_compute = ada_setup(P)
                ada_cgs = list(range(8 + 8 * hq, 16 + 8 * hq))
                ada_load(ada_cgs[0])
                for i in range(NO):
                    q = qT[i % 2]
                    bq = b_q[i % 2]
                    if i + 1 < NO:
                        mq_load(i + 1)
                        ada_load(ada_cgs[i + 1])
                    ada_compute(ada_cgs[i], 6)
                    for h in range(4):
                        S.op("pe", lambda e, h=h, q=q: e.matmul(pf[5][:, h * 16:(h + 1) * 16], lhsT=q[:, h, :], rhs=kmb[:, h, :],
                                                               start=(h == 0), stop=(h == 3)), r=[bq, b_km], w=[G_pf[5]])
                    S.op("dve", lambda e, i=i: e.tensor_tensor(
                        out=scm[:], in0=pf[5][:, 0:64].rearrange("p (h n) -> p h n", h=4),
                        in1=addm[:, i, :].unsqueeze(1).to_broadcast([128, 4, 16]), op=ALU.add), r=[G_pf[5], b_c], w=[b_scm])
                    for h in range(4):
                        S.op("dve", lambda e, h=h: e.max(out=m8[:, h, :], in_=scm[:, h, :]), r=[b_scm], w=[b_m8])
                    S.op("dve", lambda e: e.tensor_tensor(out=selm[:], in0=scm[:], in1=m8[:, :, 3:4].to_broadcast([128, 4, 16]),
                                                          op=ALU.is_ge), r=[b_scm, b_m8], w=[b_selm])
                    S.op("dve", lambda e: e.tensor_scalar(negb[:], selm[:], -1.0, -NEG, op0=ALU.add, op1=ALU.mult),
                         r=[b_selm], w=[b_negb])
                    for h in range(4):
                        S.op("pe", lambda e, h=h: e.transpose(pb[0][0:16, h * 128:(h + 1) * 128], negb[:, h, :], ident[:]),
                             r=[b_negb, G_ident], w=[G_pb[0]])
                    S.op("dve", lambda e: e.tensor_copy(negT[:], pb[0][0:16, 0:512]), r=[G_pb[0]], w=[b_negT])
                    nk = 4 * i + 4

                    def mo_S(kt):
                        sl = kt % 3
                        pt = pf[sl]
                        ii, qq = i, q
                        for h in range(4):
                            S.op("pe", lambda e, h=h: e.matmul(
                                pt[:, h * 128:(h + 1) * 128], lhsT=kT4[:, h, kt * 128:(kt + 1) * 128], rhs=qq[:, h, :],
                                start=(h == 0), stop=False), r=[b_kv, bq], w=[G_pf[sl]])
                        S.op("pe", lambda e: e.matmul(pt[:], lhsT=xmo[:, kt * 128:(kt + 1) * 128], rhs=negT[:],
                                                      start=False, stop=(kt < 4 * ii)),
                             r=[b_c, b_negT], w=[G_pf[sl]])
                        if kt >= 4 * ii:
                            S.op("pe", lambda e: e.matmul(
                                pt[:], lhsT=ident[:], rhs=cmk[:, kt - 4 * ii, :].unsqueeze(1).to_broadcast([128, 4, 128]),
                                start=False, stop=True), r=[G_ident, b_c], w=[G_pf[sl]])
                        S.op("act", lambda e: e.activation(out=Et[sl][:], in_=pt[:], func=AF.Exp, scale=SCALE),
                             r=[G_pf[sl]], w=[b_Et[sl]])

                    def mo_PV(kt):
                        sl = kt % 3
                        nk_ = nk
                        for h in range(4):
                            S.op("pe", lambda e, h=h: e.matmul(
                                Ov(pf[3 + h // 2], 129)[:, h % 2, :], lhsT=Et[sl][:, h * 128:(h + 1) * 128], rhs=v4[:, h, kt, :],
                                start=(kt == 0 and h % 2 == 0), stop=(kt == nk_ - 1 and h % 2 == 1)),
                                r=[b_Et[sl], b_kv], w=[G_pf[3 + h // 2]])

                    mo_S(0)
                    if nk > 1:
                        mo_S(1)
                    for kt in range(nk):
                        if kt + 2 < nk:
                            mo_S(kt + 2)
                        mo_PV(kt)
                    for hh in range(2):
                        S.op("dve", lambda e, hh=hh: e.tensor_copy(Om[:, 2 * hh:2 * hh + 2, :], Ov(pf[3 + hh], 129)),
                             r=[G_pf[3 + hh]], w=[b_Om])
                    S.op("dve", lambda e: e.tensor_scalar(rinv[:], Om[:, :, 128], 1e-30, None, op0=ALU.add), r=[b_Om], w=[b_ri])
                    S.op("dve", lambda e: e.reciprocal(rinv[:], rinv[:]), r=[b_ri], w=[b_ri])
                    S.op("dve", lambda e: e.tensor_tensor(out=o4[:], in0=Om[:, :, 0:128],
                                                          in1=rinv[:].unsqueeze(2).to_broadcast([128, 4, 128]), op=ALU.mult),
                         r=[b_Om, b_ri], w=[b_o4])
                    head_out(o4[:], gout[:].rearrange("p (h d) -> p h d", h=4), ob, sq, hs, kn, b_o4, b_ob, bs,
                             8 + 4 * hq, i, stg, b_stg)
                stop = phase_done()

        if not stop:
            with ExitStack() as ps:
                P = ps.enter_context
                psum_alloc(P, 6, 2)
                oTb = P(sbt("oTb", [128, 16, 512], BF16))
                rows = [P(sbt("rows%d" % k, [128, D], F32)) for k in range(2)]
                x1 = P(sbt("x1", [128, 4, D], F32))
                wbuf = [P(sbt("wbuf%d" % k, [128, 16, 256], BF16)) for k in range(4)]
                hT2 = P(sbt("hT2", [128, 16, 512], BF16))
                actT = P(sbt("actT", [128, NHC, 512], BF16))
                t1 = P(sbt("t1f", [128, D], F32))
                hb = P(sbt("hbf", [128, D], BF16))
                wfo = [P(sbt("wfo%d" % k, [128, 11, 256], BF16)) for k in range(2)]
                ss = P(sbt("ssf", [128, 4], F32))
                sg = P(sbt("sg", [128, 512], F32))
                b_oTb, b_rows, b_x1, b_wbuf, b_hT2, b_actT, b_t1, b_hb, b_wfo, b_ss, b_sg = (
                    S.buf("oTb"), S.bufs_n("rows", 2), S.bufs_n("x1_", 4), S.bufs_n("wbuf", 4), S.buf("hT2"), S.buf("actT"),
                    S.buf("t1"), S.buf("hb"), S.bufs_n("wfo", 2), S.buf("ss"), S.buf("sg"))
                for tb in range(2):
                    S.dma("sp", oTb[:], OT[:, :, tb * 512:(tb + 1) * 512].rearrange("h d t -> d h t"), r=G_OT, w=[b_oTb])
                    S.dma("sp", rows[0][:], modscr[:, 2 * D:3 * D], r=[G_mod], w=[b_rows[0]])
                    for tt in range(4):
                        S.dma("sp", x1[:, tt, :], x_own[(tb * 4 + tt) * 128:(tb * 4 + tt + 1) * 128, :], w=[b_x1[tt]])
                    w_out_v = w_out.rearrange("(h p) n -> p h n", p=128)
                    wn = 0
                    S.dma("pool", wbuf[0][:], w_out_v[:, :, 0:256], w=[b_wbuf[0]])
                    for cg in range(8):
                        sl = cg % 2
                        if cg + 1 < 8:
                            S.dma("pool", wbuf[1 - sl][:], w_out_v[:, :, (cg + 1) * 256:(cg + 2) * 256], w=[b_wbuf[1 - sl]])
                        for tt in range(4):
                            bk = 4 + tt % 2
                            for h in range(16):
                                S.op("pe", lambda e, bk=bk, h=h, tt=tt, sl=sl: e.matmul(
                                    pf[bk][:, 0:256], lhsT=oTb[:, h, tt * 128:(tt + 1) * 128], rhs=wbuf[sl][:, h, :],
                                    start=(h == 0), stop=(h == 15)), r=[b_oTb, b_wbuf[sl]], w=[G_pf[bk]])
                            S.op("dve", lambda e, bk=bk, cg=cg: e.tensor_tensor(out=t1[:, 0:256], in0=pf[bk][:, 0:256],
                                                                               in1=rows[0][:, cg * 256:(cg + 1) * 256], op=ALU.mult),
                                 r=[G_pf[bk], b_rows[0]], w=[b_t1])
                            S.op("dve", lambda e, tt=tt, cg=cg: e.tensor_tensor(out=x1[:, tt, cg * 256:(cg + 1) * 256],
                                                                               in0=x1[:, tt, cg * 256:(cg + 1) * 256],
                                                                               in1=t1[:, 0:256], op=ALU.add),
                                 r=[b_t1, b_x1[tt]], w=[b_x1[tt]])
                    S.dma("sp", rows[1][:], modscr[:, 4 * D:5 * D], r=[G_mod], w=[b_rows[1]])
                    S.dma("sp", rows[0][:], modscr[:, 3 * D:4 * D], r=[G_mod], w=[b_rows[0]])
                    for tt in range(4):
                        S.op("act", lambda e, tt=tt: e.activation(out=hb[:], in_=x1[:, tt, :], func=AF.Square, accum_out=ss[:, 0:1]),
                             r=[b_x1[tt]], w=[b_hb, b_ss])
                        S.op("act", lambda e: e.activation(out=ss[:, 1:2], in_=ss[:, 0:1], func=AF.Sqrt, scale=1.0 / D, bias=1e-6),
                             r=[b_ss], w=[b_ss])
                        S.op("dve", lambda e: e.reciprocal(ss[:, 2:3], ss[:, 1:2]), r=[b_ss], w=[b_ss])
                        S.op("dve", lambda e, tt=tt: e.scalar_tensor_tensor(out=t1[:], in0=x1[:, tt, :], scalar=ss[:, 2:3],
                                                                            in1=rows[1][:], op0=ALU.mult, op1=ALU.mult),
                             r=[b_x1[tt], b_ss, b_rows[1]], w=[b_t1])
                        S.op("dve", lambda e: e.tensor_tensor(out=hb[:], in0=t1[:], in1=rows[0][:], op=ALU.add),
                             r=[b_t1, b_rows[0]], w=[b_hb])
                        for half in range(2):
                            for q_ in range(8):
                                kc = half * 8 + q_
                                S.op("pe", lambda e, half=half, q_=q_, kc=kc: e.transpose(
                                    pb[half][:, q_ * 128:(q_ + 1) * 128], hb[:, kc * 128:(kc + 1) * 128], ident[:]),
                                    r=[b_hb, G_ident], w=[G_pb[half]])
                            S.op("dve" if half else "act", (lambda e, half=half, tt=tt: e.tensor_copy(
                                hT2[:, half * 8:(half + 1) * 8, tt * 128:(tt + 1) * 128],
                                pb[half][:].rearrange("p (a b) -> p a b", a=8))) if half else
                                (lambda e, half=half, tt=tt: e.activation(
                                    out=hT2[:, half * 8:(half + 1) * 8, tt * 128:(tt + 1) * 128],
                                    in_=pb[half][:].rearrange("p (a b) -> p a b", a=8), func=AF.Copy)),
                                r=[G_pb[half]], w=[b_hT2])
                    w_fi_v = w_fi.rearrange("(kc p) n -> p kc n", p=128)
                    S.dma("pool", wbuf[0][:], w_fi_v[:, :, 0:256], w=[b_wbuf[0]])
                    S.dma("pool", wbuf[1][:], w_fi_v[:, :, HID:HID + 256], w=[b_wbuf[1]])
                    for gi in range(22):
                        sl = (gi % 2) * 2
                        if gi + 1 < 22:
                            S.dma("pool", wbuf[2 - sl][:], w_fi_v[:, :, (gi + 1) * 256:(gi + 2) * 256], w=[b_wbuf[2 - sl]])
                            S.dma("pool", wbuf[3 - sl][:], w_fi_v[:, :, HID + (gi + 1) * 256:HID + (gi + 2) * 256],
                                  w=[b_wbuf[3 - sl]])
                        for hl in range(2):
                            hc = gi * 2 + hl
                            pg, pu = hc % 2, 2 + hc % 2
                            for kc in range(16):
                                S.op("pe", lambda e, pg=pg, sl=sl, kc=kc, hl=hl: e.matmul(
                                    pf[pg][:], lhsT=wbuf[sl][:, kc, hl * 128:(hl + 1) * 128], rhs=hT2[:, kc, :],
                                    start=(kc == 0), stop=(kc == 15)), r=[b_wbuf[sl], b_hT2], w=[G_pf[pg]])
                            for kc in range(16):
                                S.op("pe", lambda e, pu=pu, sl=sl, kc=kc, hl=hl: e.matmul(
                                    pf[pu][:], lhsT=wbuf[sl + 1][:, kc, hl * 128:(hl + 1) * 128], rhs=hT2[:, kc, :],
                                    start=(kc == 0), stop=(kc == 15)), r=[b_wbuf[sl + 1], b_hT2], w=[G_pf[pu]])
                            S.op("act", lambda e, pg=pg: e.activation(out=sg[:], in_=pf[pg][:], func=AF.Silu), r=[G_pf[pg]], w=[b_sg])
                            S.op("dve", lambda e, pu=pu, hc=hc: e.tensor_tensor(out=actT[:, hc, :], in0=sg[:], in1=pf[pu][:],
                                                                               op=ALU.mult), r=[b_sg, G_pf[pu]], w=[b_actT])
                    S.dma("sp", rows[1][:], modscr[:, 5 * D:6 * D], r=[G_mod], w=[b_rows[1]])
                    w_fo_v = w_fo.rearrange("(hc p) n -> p hc n", p=128)
                    S.dma("pool", wfo[0][:], w_fo_v[:, 0:11, 0:256], w=[b_wfo[0]])
                    pn = 0
                    for cg in range(8):
                        for piece in range(4):
                            sl = pn % 2
                            pn += 1
                            nxt = pn
                            if nxt < 32:
                                cgn, pcn = nxt // 4, nxt % 4
                                S.dma("pool", wfo[1 - sl][:], w_fo_v[:, pcn * 11:(pcn + 1) * 11, cgn * 256:(cgn + 1) * 256],
                                      w=[b_wfo[1 - sl]])
                            for hl in range(11):
                                hc = piece * 11 + hl
                                for tt in range(4):
                                    S.op("pe", lambda e, tt=tt, hc=hc, hl=hl, sl=sl: e.matmul(
                                        pf[tt][:, 0:256], lhsT=actT[:, hc, tt * 128:(tt + 1) * 128], rhs=wfo[sl][:, hl, :],
                                        start=(hc == 0), stop=(hc == NHC - 1)), r=[b_actT, b_wfo[sl]], w=[G_pf[tt]])
                        for tt in range(4):
                            S.op("dve", lambda e, tt=tt, cg=cg: e.tensor_tensor(out=t1[:, 0:256], in0=pf[tt][:, 0:256],
                                                                               in1=rows[1][:, cg * 256:(cg + 1) * 256], op=ALU.mult),
                                 r=[G_pf[tt], b_rows[1]], w=[b_t1])
                            S.op("dve", lambda e, tt=tt, cg=cg: e.tensor_tensor(out=x1[:, tt, cg * 256:(cg + 1) * 256],
                                                                               in0=x1[:, tt, cg * 256:(cg + 1) * 256],
                                                                               in1=t1[:, 0:256], op=ALU.add),
                                 r=[b_t1, b_x1[tt]], w=[b_x1[tt]])
                    for tt in range(4):
                        S.dma("sp", y_out[(tb * 4 + tt) * 128:(tb * 4 + tt + 1) * 128, :], x1[:, tt, :], r=[b_x1[tt]], w=[G_y])
                stop = phase_done()

        S.flush(barrier=True)
        if os.environ.get("K_VERBOSE"):
            print("n_inst", S.n_inst, "phases", phase[0], "counts", {str(k): v for k, v in S.cnt.items()})
    return nc


def kernel(**inputs):
    f32 = np.float32
    x = np.asarray(inputs["x"], f32)
    c = np.asarray(inputs["c"], f32)
    nc = build_nc()
    in_maps = []
    consts = [_consts(j) for j in range(4)]
    rep = lambda v: np.ascontiguousarray(np.broadcast_to(np.asarray(v, f32), (128,) + np.asarray(v).shape))
    shared = {
        "w_ada": np.ascontiguousarray(inputs["w_ada"][0]),
        "b_ada": np.ascontiguousarray(inputs["b_ada"][0][None, :]),
        "w_in": np.ascontiguousarray(inputs["w_in"][0]),
        "g_nq": rep(inputs["nsa_q_norm"][0]),
        "g_nk": rep(inputs["nsa_k_norm"][0]),
        "g_mq": rep(inputs["moba_q_norm"][0]),
        "g_mk": rep(inputs["moba_k_norm"][0]),
        "pe_k": np.ascontiguousarray(inputs["cmp_pe_k"][0]),
        "w1_k": np.ascontiguousarray(inputs["cmp_w1_k"][0]),
        "w2_k": np.ascontiguousarray(inputs["cmp_w2_k"][0]),
        "pe_v": np.ascontiguousarray(inputs["cmp_pe_v"][0]),
        "w1_v": np.ascontiguousarray(inputs["cmp_w1_v"][0]),
        "w2_v": np.ascontiguousarray(inputs["cmp_w2_v"][0]),
        "g_out": rep(inputs["out_norm"][0]),
        "w_out": np.ascontiguousarray(inputs["w_out"][0]),
        "w_fi": np.ascontiguousarray(inputs["w_ffn_in"][0]),
        "w_fo": np.ascontiguousarray(inputs["w_ffn_out"][0]),
        "ident32": np.eye(32, dtype=f32),
    }
    shared = {k: np.asarray(v, f32) if v.dtype != ml_dtypes.bfloat16 else v for k, v in shared.items()}
    for b in range(2):
        for j in range(4):
            cj = consts[j]
            m = dict(shared)
            m["x_full"] = np.ascontiguousarray(x[b])
            m["x_own"] = np.ascontiguousarray(x[b][cj["own_tok"]])
            m["c_col"] = np.ascontiguousarray(c[b].reshape(16, 128).T)
            for k in ("cs_full", "cs_own", "cs_cmp", "cmpmask", "cmask", "wmask", "addsel", "addmoba", "xsel",
                      "xmoba", "ident", "ovaug"):
                m[k] = cj[k]
            in_maps.append(m)
    res = run_bass_kernel_spmd(nc, in_maps, core_ids=list(range(8)))
    out = np.zeros((2, T, D), f32)
    for b in range(2):
        for j in range(4):
            r = res.results[4 * b + j]
            out[b][consts[j]["own_tok"]] = r["y_out"]
    if KDEBUG:
        kernel.last = res.results
    return out
```

```python
import os
import numpy as np
import ml_dtypes
from contextlib import ExitStack
import concourse.bass as bass
import concourse.mybir as mybir
from concourse.bass_utils import run_bass_kernel_spmd

F32 = mybir.dt.float32
BF16 = mybir.dt.bfloat16
ALU = mybir.AluOpType
AF = mybir.ActivationFunctionType
AX = mybir.AxisListType

T = 4096
D = 2048
NT = 32
NO = 8
HID = 5632
NHC = HID // 128
SCALE = 128.0 ** -0.5
NEG = -30000.0
SAME_ENG_SYNC = os.environ.get("K_SES", "1") == "1"
NPHASE = int(os.environ.get("K_NPHASE", "99"))
KDEBUG = os.environ.get("K_DEBUG", "0") == "1"


class Buf:
    __slots__ = ("name", "lw", "rd")

    def __init__(self, name):
        self.name = name
        self.lw = None
        self.rd = []


class Op:
    __slots__ = ("eng", "fn", "r", "w", "deps", "signal", "tok", "dma")

    def __init__(self, eng, fn, r, w, dma=False):
        self.eng, self.fn, self.r, self.w = eng, fn, r, w
        self.deps = set()
        self.signal = False
        self.tok = None
        self.dma = dma


class Sched:
    NSLOT = 8

    def __init__(self, nc, es):
        self.nc = nc
        self.h = {"pe": nc.tensor, "act": nc.scalar, "dve": nc.vector, "pool": nc.gpsimd, "sp": nc.sync}
        self.sem = {}
        self.cnt = {}
        for e in ("pe", "act", "dve", "pool"):
            self.sem[e] = es.enter_context(nc.semaphore("s_" + e))
            self.cnt[e] = 0
        self.dq = {}
        for q in ("sp", "pool"):
            for s in range(self.NSLOT):
                k = ("dma", q, s)
                self.sem[k] = es.enter_context(nc.semaphore("d_%s%d" % (q, s)))
                self.cnt[k] = 0
            self.dq[q] = 0
        self.waited = {e: {} for e in self.h}
        self.ops = []
        self.bufs = []
        self.n_inst = 0

    def buf(self, name):
        b = Buf(name)
        self.bufs.append(b)
        return b

    def bufs_n(self, name, n):
        return [self.buf("%s%d" % (name, i)) for i in range(n)]

    def op(self, eng, fn, r=(), w=()):
        self.ops.append(Op(eng, fn, tuple(r), tuple(w)))

    def dma(self, q, out, in_, r=(), w=()):
        self.ops.append(Op(q, (out, in_), tuple(r), tuple(w), dma=True))

    def _wait(self, eng, key, val):
        wd = self.waited[eng]
        if wd.get(key, 0) >= val:
            return
        self.h[eng].wait_ge(self.sem[key], val)
        wd[key] = val
        self.n_inst += 1

    def flush(self, barrier=True):
        ops = self.ops
        for i, o in enumerate(ops):
            for b in o.r:
                if b.lw is not None:
                    o.deps.add(b.lw)
            for b in o.w:
                if b.lw is not None:
                    o.deps.add(b.lw)
                o.deps.update(b.rd)
            o.deps.discard(i)
            for b in o.r:
                if not o.dma:
                    b.rd = [x for x in b.rd if ops[x].dma or ops[x].eng != o.eng]
                b.rd.append(i)
            for b in o.w:
                b.lw = i
                b.rd = []
        for i, o in enumerate(ops):
            keep = set()
            for d in o.deps:
                p = ops[d]
                if (not p.dma) and p.eng == o.eng and (not o.dma):
                    if p.eng == "pe" or not SAME_ENG_SYNC:
                        continue
                p.signal = True
                keep.add(d)
            o.deps = keep
        last = {}
        for i, o in enumerate(ops):
            if not o.dma:
                last[o.eng] = i
        if barrier:
            for e, i in last.items():
                ops[i].signal = True
        for i, o in enumerate(ops):
            for d in sorted(o.deps):
                k, v = ops[d].tok
                self._wait(o.eng, k, v)
            if o.dma:
                q = o.eng
                s = self.dq[q] % self.NSLOT
                self.dq[q] += 1
                k = ("dma", q, s)
                if self.cnt[k] > 0:
                    self._wait(q, k, self.cnt[k])
                out, in_ = o.fn
                ins = self.h[q].dma_start(out=out, in_=in_)
                self.cnt[k] += 16
                ins.then_inc(self.sem[k], 16)
                o.tok = (k, self.cnt[k])
            else:
                ins = o.fn(self.h[o.eng])
                if o.signal:
                    self.cnt[o.eng] += 1
                    ins.then_inc(self.sem[o.eng], 1)
                    o.tok = (o.eng, self.cnt[o.eng])
            self.n_inst += 1
        if barrier:
            for e in self.h:
                for k in self.sem:
                    if k == e:
                        continue
                    if self.cnt[k] > 0:
                        self._wait(e, k, self.cnt[k])
        self.ops = []
        for b in self.bufs:
            b.lw = None
            b.rd = []
        self.bufs = [b for b in self.bufs if getattr(b, "name", "").startswith("G_")]


def _bf(a):
    return np.ascontiguousarray(a.astype(ml_dtypes.bfloat16))


def _rope_tab(pos):
    inv = (500000.0 ** (-np.arange(0, 32, 2, dtype=np.float32) / np.float32(32))).astype(np.float32)
    ang = pos.astype(np.float32)[:, None] * inv[None, :]
    return np.concatenate([np.cos(ang), np.sin(ang)], axis=1).astype(np.float32)


def _consts(j):
    c = {}
    own_tok = np.concatenate([np.arange(128 * (4 * i + j), 128 * (4 * i + j) + 128) for i in range(NO)])
    c["own_tok"] = own_tok
    cs_full = _rope_tab(np.arange(T))
    c["cs_full"] = np.ascontiguousarray(cs_full.reshape(NT, 128, 32).transpose(1, 0, 2))
    c["cs_own"] = np.ascontiguousarray(cs_full[own_tok].reshape(NO, 128, 32).transpose(1, 0, 2))
    cmp_end = np.arange(256) * 16 + 31
    cmp_end[255] = 0
    cs_cmp = _rope_tab(cmp_end)
    c["cs_cmp"] = np.ascontiguousarray(cs_cmp.reshape(2, 128, 32).transpose(1, 0, 2))
    cmp_end = np.arange(256) * 16 + 31
    kk = np.arange(128)[:, None]
    tt = np.arange(128)[None, :]
    cm = np.zeros((128, NO, 2, 128), np.float32)
    for i in range(NO):
        qb = 4 * i + j
        for ct in range(2):
            cidx = ct * 128 + kk
            vis = (cmp_end[cidx] <= (128 * qb + tt)) & (cidx < 255)
            cm[:, i, ct, :] = np.where(vis, 0.0, NEG)
    c["cmpmask"] = _bf(cm)
    cmk = np.zeros((128, 4, 128), np.float32)
    for jj in range(4):
        if jj < j:
            cmk[:, jj, :] = 0.0
        elif jj == j:
            cmk[:, jj, :] = np.where(kk <= tt, 0.0, NEG)
        else:
            cmk[:, jj, :] = NEG
    c["cmask"] = _bf(cmk)
    wm = np.zeros((128, 8, 128), np.float32)
    for w in range(8):
        dl = w - 4 - j
        if dl == 0:
            wm[:, w, :] = np.where(kk <= tt, 0.0, NEG)
        elif dl in (-1, -2, -3):
            wm[:, w, :] = 0.0
        elif dl == -4:
            wm[:, w, :] = np.where(kk > tt, 0.0, NEG)
        else:
            wm[:, w, :] = NEG
    c["wmask"] = _bf(wm)
    am = np.zeros((128, NO, 64), np.float32)
    amm = np.zeros((128, NO, 16), np.float32)
    for i in range(NO):
        qb = 4 * i + j
        pos = 128 * qb + np.arange(128)
        cur = pos // 64
        blk = np.arange(64)[None, :]
        forced = (blk == 0) | (blk == cur[:, None]) | (blk == cur[:, None] - 1)
        valid = blk <= cur[:, None]
        am[:, i, :] = np.where(valid, np.where(forced, 8.0, 0.0), -1e30)
        curm = pos // 256
        blkm = np.arange(16)[None, :]
        amm[:, i, :] = np.where(blkm < curm[:, None], 0.0, np.where(blkm == curm[:, None], 1e30, -1e30))
    c["addsel"] = am
    c["addmoba"] = amm
    xs = np.zeros((64, T), np.float32)
    xs[np.arange(T) // 64, np.arange(T)] = 1.0
    c["xsel"] = _bf(xs)
    xm = np.zeros((16, T), np.float32)
    xm[np.arange(T) // 256, np.arange(T)] = 1.0
    c["xmoba"] = _bf(xm)
    c["ident"] = _bf(np.eye(128, dtype=np.float32))
    cs_ = np.arange(256) * 16
    sb = np.arange(64) * 64
    ov = ((cs_[:, None] < sb[None, :] + 64) & (cs_[:, None] + 32 > sb[None, :])).astype(np.float32)
    oa = np.concatenate([np.ones((256, 1), np.float32), ov], axis=1)
    oa[255] = 0.0
    c["ovaug"] = _bf(oa.reshape(2, 128, 65).transpose(1, 0, 2))
    return c


def build_nc():
    nc = bass.Bass("TRN2", target_bir_lowering=False)

    uid = [0]

    def sbt(name, shape, dt):
        uid[0] += 1
        return nc.sbuf_tensor("%s_%d" % (name, uid[0]), shape, dt)

    def din(name, shape, dt=F32):
        return nc.dram_tensor(name, list(shape), dt, kind="ExternalInput").ap()

    def dscr(name, shape, dt):
        kind = "ExternalOutput" if KDEBUG else "Internal"
        return nc.dram_tensor(name, list(shape), dt, kind=kind).ap()

    x_full = din("x_full", [T, D])
    x_own = din("x_own", [1024, D])
    c_col = din("c_col", [128, 16])
    w_ada = din("w_ada", [D, 6 * D])
    b_ada = din("b_ada", [1, 6 * D])
    w_in = din("w_in", [D, 5656])
    g_nq = din("g_nq", [128, 128])
    g_nk = din("g_nk", [128, 3, 128])
    g_mq = din("g_mq", [128, 128])
    g_mk = din("g_mk", [128, 128])
    pe_k = din("pe_k", [32, 128])
    w1_k = din("w1_k", [4096, 128])
    w2_k = din("w2_k", [128, 128])
    pe_v = din("pe_v", [32, 128])
    w1_v = din("w1_v", [4096, 128])
    w2_v = din("w2_v", [128, 128])
    g_out = din("g_out", [128, D])
    w_out = din("w_out", [D, D])
    w_fi = din("w_fi", [D, 2 * HID])
    w_fo = din("w_fo", [HID, D])
    cs_full = din("cs_full", [128, NT, 32])
    cs_own = din("cs_own", [128, NO, 32])
    cs_cmp = din("cs_cmp", [128, 2, 32])
    cmpmask_d = din("cmpmask", [128, NO, 2, 128], BF16)
    cmask_d = din("cmask", [128, 4, 128], BF16)
    wmask_d = din("wmask", [128, 8, 128], BF16)
    addsel_d = din("addsel", [128, NO, 64])
    addmoba_d = din("addmoba", [128, NO, 16])
    xsel_d = din("xsel", [64, T], BF16)
    xmoba_d = din("xmoba", [16, T], BF16)
    ident_d = din("ident", [128, 128], BF16)
    ident32_d = din("ident32", [32, 32])
    ovaug_d = din("ovaug", [128, 2, 65], BF16)
    y_out = nc.dram_tensor("y_out", [1024, D], F32, kind="ExternalOutput").ap()

    modscr = dscr("modscr", [128, 6 * D], F32)
    KT = dscr("KT", [16, 128, T], BF16)
    VV = dscr("VV", [12, T, 128], BF16)
    QT = dscr("QT", [16, 128, 1024], BF16)
    OT = dscr("OT", [16, 128, 1024], BF16)
    GS = dscr("GS", [128, NO, 24], F32)

    with ExitStack() as es:
        E = es.enter_context
        S = Sched(nc, es)
        pf, pb, G_pf, G_pb = [], [], [], []
        G_pbh = [None]

        def psum_alloc(P, nf=6, nb=2):
            uid[0] += 1
            pf[:] = [P(nc.psum_tensor("pf%d_%d" % (i, uid[0]), [128, 512], F32)) for i in range(nf)]
            pb[:] = [P(nc.psum_tensor("pb%d_%d" % (i, uid[0]), [128, 1024], BF16)) for i in range(nb)]
            G_pf[:] = [S.buf("pf%d" % i) for i in range(nf)]
            G_pb[:] = [S.buf("pb%d" % i) for i in range(nb)]
            G_pbh[0] = S.buf("pbh")

        ident = E(sbt("ident_s", [128, 128], BF16))
        G_ident = S.buf("G_ident")
        G_mod = S.buf("G_modscr")
        G_KT = [S.buf("G_KT%d" % i) for i in range(16)]
        G_VV = [S.buf("G_VV%d" % i) for i in range(12)]
        G_QT = [S.buf("G_QT%d" % i) for i in range(16)]
        G_OT = [S.buf("G_OT%d" % i) for i in range(16)]
        G_GS = S.buf("G_GS")
        G_y = S.buf("G_y")
        S.dma("sp", ident[:], ident_d[:, :], w=[G_ident])

        phase = [0]

        def phase_done():
            S.flush(barrier=True)
            phase[0] += 1
            return phase[0] >= NPHASE

        w_ada_v = w_ada.rearrange("(kc p) n -> p kc n", p=128)

        def ada_setup(P):
            ccol = P(sbt("ccol", [128, 16], F32))
            scb = P(sbt("scb", [128, 16], BF16))
            screp = P(sbt("screp", [128, 16, 128], BF16))
            wa = [P(sbt("wa%d" % i, [128, 16, 512], BF16)) for i in range(2)]
            br = [P(sbt("br%d" % i, [128, 512], F32)) for i in range(2)]
            mrow = [P(sbt("mrow%d" % i, [128, 512], F32)) for i in range(2)]
            b_ccol, b_scb, b_screp = S.buf("ccol"), S.buf("scb"), S.buf("screp")
            b_wa, b_br, b_mrow = S.bufs_n("wa", 2), S.bufs_n("br", 2), S.bufs_n("mrow", 2)
            S.dma("sp", ccol[:], c_col[:, :], w=[b_ccol])
            S.op("act", lambda e: e.activation(out=scb[:], in_=ccol[:], func=AF.Silu), r=[b_ccol], w=[b_scb])
            S.op("dve", lambda e: e.tensor_copy(screp[:], scb[:].unsqueeze(2).to_broadcast([128, 16, 128])),
                 r=[b_scb], w=[b_screp])

            def load(cg):
                sl = cg % 2
                S.dma("pool", wa[sl][:], w_ada_v[:, :, cg * 512:(cg + 1) * 512], w=[b_wa[sl]])
                S.dma("sp", br[sl][:], b_ada[0:1, cg * 512:(cg + 1) * 512].partition_broadcast(128), w=[b_br[sl]])

            def compute(cg, bank):
                sl = cg % 2
                pt = pf[bank]
                gb = G_pf[bank]
                for kc in range(16):
                    S.op("pe", lambda e, kc=kc: e.matmul(pt[:], lhsT=screp[:, kc, :], rhs=wa[sl][:, kc, :],
                                                         start=(kc == 0), stop=(kc == 15)),
                         r=[b_screp, b_wa[sl]], w=[gb])
                addc = 1.0 if (cg // 4) in (1, 4) else 0.0
                S.op("dve", lambda e: e.scalar_tensor_tensor(out=mrow[sl][:], in0=pt[:], scalar=addc, in1=br[sl][:],
                                                             op0=ALU.add, op1=ALU.add),
                     r=[gb, b_br[sl]], w=[b_mrow[sl]])
                S.dma("sp", modscr[:, cg * 512:(cg + 1) * 512], mrow[sl][:], r=[b_mrow[sl]], w=[G_mod])

            return load, compute

        with ExitStack() as ps:
            P = ps.enter_context
            psum_alloc(P, 6, 2)
            ada_load, ada_compute = ada_setup(P)
            ada_load(0)
            for cg in range(8):
                if cg + 1 < 8:
                    ada_load(cg + 1)
                ada_compute(cg, cg % 2)
            stop = phase_done()

        def KG(ap3, i=None):
            return ap3

        def run_proj_all(supers):
            ntile = 8
            with ExitStack() as ps:
                P = ps.enter_context
                psum_alloc(P, 6, 2)
                m1 = P(sbt("m1", [128, D], F32))
                m2 = P(sbt("m2", [128, D], F32))
                cs = P(sbt("cs", [128, NT + NO, 32], F32))
                gq = P(sbt("gq", [128, 128], F32))
                gk = P(sbt("gk", [128, 3, 128], F32))
                gmq = P(sbt("gmq", [128, 128], F32))
                gmk = P(sbt("gmk", [128, 128], F32))
                xt = [P(sbt("xt%d" % i, [128, D], F32)) for i in range(2)]
                junk = P(sbt("junk", [128, D], BF16))
                t1 = P(sbt("t1", [128, D], F32))
                hb = P(sbt("hb", [128, D], BF16))
                ss = P(sbt("ss", [128, 4], F32))
                hT = P(sbt("hT", [128, 16, ntile * 128], BF16))
                wb = [P(sbt("wb%d" % i, [128, 16, 512], BF16)) for i in range(2)]
                sqL = [P(sbt("sq", [128, 512], BF16)) for _ in range(3)]
                knL = [P(sbt("kn", [128, 4, 128], F32)) for _ in range(3)]
                kbL = [P(sbt("kb", [128, 4, 128], BF16)) for _ in range(3)]
                hsL = [P(sbt("hs", [128, 8], F32)) for _ in range(3)]
                rpL = [P(sbt("rp", [128, 4, 4, 16], F32)) for _ in range(3)]
                stT = [P(sbt("stT%d" % i, [128, 4, ntile * 128], BF16)) for i in range(2)]
                stV = [P(sbt("stV%d" % i, [128, ntile, 4, 128], BF16)) for i in range(2)]
                gsg = P(sbt("gsg", [128, NO, 24], F32))
                b_m, b_cs, b_g = S.buf("m"), S.buf("cs"), S.buf("g")
                b_xt = S.bufs_n("xt", 2)
                b_junk, b_t1, b_hb, b_ss = S.buf("junk"), S.buf("t1"), S.buf("hb"), S.buf("ss")
                b_hTt = S.bufs_n("hTt", ntile)
                b_wb = S.bufs_n("wb", 2)
                b_sqL, b_knL, b_kbL, b_hsL, b_rpL = (S.bufs_n("sq", 3), S.bufs_n("kn", 3), S.bufs_n("kb", 3),
                                                     S.bufs_n("hs", 3), S.bufs_n("rp", 3))
                b_stT, b_stV = S.bufs_n("stT", 2), S.bufs_n("stV", 2)
                b_gsg = S.buf("gsg")
                S.dma("sp", m1[:], modscr[:, D:2 * D], r=[G_mod], w=[b_m])
                S.dma("sp", m2[:], modscr[:, 0:D], r=[G_mod], w=[b_m])
                S.dma("sp", cs[:, 0:NT, :], cs_full[:, :, :], w=[b_cs])
                S.dma("sp", cs[:, NT:NT + NO, :], cs_own[:, :, :], w=[b_cs])
                S.dma("sp", gq[:], g_nq[:, :], w=[b_g])
                S.dma("sp", gk[:], g_nk[:, :, :], w=[b_g])
                S.dma("sp", gmq[:], g_mq[:, :], w=[b_g])
                S.dma("sp", gmk[:], g_mk[:, :], w=[b_g])
                gains = {"nq": gq[:], "k1": gk[:, 1, :], "k2": gk[:, 2, :], "mq": gmq[:], "mk": gmk[:]}
                w_in_v = w_in.rearrange("(kc p) n -> p kc n", p=128)
                NS = len(supers)

                def xsrc_tile(gt):
                    return supers[gt // ntile][0][(gt % ntile) * 128:(gt % ntile + 1) * 128, :]

                S.dma("sp", xt[0][:], xsrc_tile(0), w=[b_xt[0]])

                def hTgen(gt):
                    tt = gt % ntile
                    sl = gt % 2
                    if gt + 1 < NS * ntile:
                        S.dma("sp", xt[1 - sl][:], xsrc_tile(gt + 1), w=[b_xt[1 - sl]])
                    S.op("act", lambda e: e.activation(out=junk[:], in_=xt[sl][:], func=AF.Square, accum_out=ss[:, 0:1]),
                         r=[b_xt[sl]], w=[b_junk, b_ss])
                    S.op("act", lambda e: e.activation(out=ss[:, 1:2], in_=ss[:, 0:1], func=AF.Sqrt, scale=1.0 / D, bias=1e-6),
                         r=[b_ss], w=[b_ss])
                    S.op("dve", lambda e: e.reciprocal(ss[:, 2:3], ss[:, 1:2]), r=[b_ss], w=[b_ss])
                    S.op("dve", lambda e: e.scalar_tensor_tensor(out=t1[:], in0=xt[sl][:], scalar=ss[:, 2:3], in1=m1[:],
                                                                 op0=ALU.mult, op1=ALU.mult),
                         r=[b_xt[sl], b_ss, b_m], w=[b_t1])
                    S.op("dve", lambda e: e.tensor_tensor(out=hb[:], in0=t1[:], in1=m2[:], op=ALU.add),
                         r=[b_t1, b_m], w=[b_hb])
                    for half in range(2):
                        for q_ in range(8):
                            kc = half * 8 + q_
                            S.op("pe", lambda e, half=half, q_=q_, kc=kc: e.transpose(
                                pb[half][:, q_ * 128:(q_ + 1) * 128], hb[:, kc * 128:(kc + 1) * 128], ident[:]),
                                r=[b_hb, G_ident], w=[G_pb[half]])
                        if half == 0:
                            S.op("act", lambda e, half=half: e.activation(
                                out=hT[:, half * 8:(half + 1) * 8, tt * 128:(tt + 1) * 128],
                                in_=pb[half][:].rearrange("p (a b) -> p a b", a=8), func=AF.Copy),
                                r=[G_pb[half]], w=[b_hTt[tt]])
                        else:
                            S.op("dve", lambda e, half=half: e.tensor_copy(
                                hT[:, half * 8:(half + 1) * 8, tt * 128:(tt + 1) * 128],
                                pb[half][:].rearrange("p (a b) -> p a b", a=8)),
                                r=[G_pb[half]], w=[b_hTt[tt]])

                steps = []
                for si, (xs_, cs0, groups) in enumerate(supers):
                    for gi in range(len(groups)):
                        for tt in range(ntile):
                            steps.append((si, gi, tt))
                gseq = [(si, gi) for si, (xs_, cs0, groups) in enumerate(supers) for gi in range(len(groups))]
                gidx = {sg: k for k, sg in enumerate(gseq)}

                def ginfo(si, gi):
                    c0, ncol, heads = supers[si][2][gi]
                    return c0, ncol, heads, gidx[(si, gi)] % 2

                c0_, nc_, _, _ = ginfo(0, 0)
                S.dma("pool", wb[0][:, :, 0:nc_], w_in_v[:, :, c0_:c0_ + nc_], w=[b_wb[0]])

                def stage_A(n):
                    si, gi, tt = steps[n]
                    c0, ncol, heads, sl = ginfo(si, gi)
                    if tt == 0:
                        k = gidx[(si, gi)]
                        if k + 1 < len(gseq):
                            c0n, ncn, _, sln = ginfo(*gseq[k + 1])
                            S.dma("pool", wb[sln][:, :, 0:ncn], w_in_v[:, :, c0n:c0n + ncn], w=[b_wb[sln]])
                    bk = 2 + (n % 4)
                    pt = pf[bk]
                    for kc in range(16):
                        S.op("pe", lambda e, kc=kc: e.matmul(
                            pt[:, 0:ncol], lhsT=hT[:, kc, tt * 128:(tt + 1) * 128], rhs=wb[sl][:, kc, 0:ncol],
                            start=(kc == 0), stop=(kc == 15)),
                            r=[b_hTt[tt], b_wb[sl]], w=[G_pf[bk]])
                    if gi == len(supers[si][2]) - 1 and si + 1 < NS:
                        hTgen((si + 1) * ntile + tt)

                def scr(n):
                    k3 = n % 3
                    return (sqL[k3], knL[k3], kbL[k3], hsL[k3], rpL[k3], b_sqL[k3], b_knL[k3], b_kbL[k3], b_hsL[k3], b_rpL[k3])

                def hk(heads):
                    nK = sum(1 for hh in heads if hh[0] == "K")
                    nC = sum(1 for hh in heads if hh[0] == "C")
                    return nK, nC, len(heads) - nK - nC

                def stage_B1(n):
                    si, gi, tt = steps[n]
                    c0, ncol, heads, sl = ginfo(si, gi)
                    bk = 2 + (n % 4)
                    pt = pf[bk]
                    sq, kn, kb, hs, rp, b_sq, b_kn, b_kb, b_hs, b_rp = scr(n)
                    if heads == "gates":
                        S.op("act", lambda e: e.activation(out=gsg[:, tt, :], in_=pt[:, 0:24], func=AF.Sigmoid),
                             r=[G_pf[bk]], w=[b_gsg])
                        return
                    nK, nC, nV = hk(heads)
                    for h in range(nK):
                        S.op("act", lambda e, h=h: e.activation(out=sq[:, h * 128:(h + 1) * 128], in_=pt[:, h * 128:(h + 1) * 128],
                                                                func=AF.Square, accum_out=hs[:, h:h + 1]),
                             r=[G_pf[bk]], w=[b_sq, b_hs])
                    if nK > 0:
                        S.op("act", lambda e: e.activation(out=hs[:, 4:4 + nK], in_=hs[:, 0:nK], func=AF.Sqrt, scale=1.0 / 128,
                                                           bias=1e-6), r=[b_hs], w=[b_hs])
                    if nC > 0:
                        S.op("act", lambda e: e.activation(out=kb[:, 0:nC, :], in_=pt[:, 0:nC * 128].rearrange("p (h d) -> p h d", h=nC),
                                                           func=AF.Copy), r=[G_pf[bk]], w=[b_kb])
                    if nV > 0:
                        nTr = nK + nC
                        S.op("act", lambda e: e.activation(
                            out=stV[sl][:, tt, 0:nV, :],
                            in_=pt[:, nTr * 128:(nTr + nV) * 128].rearrange("p (h d) -> p h d", h=nV), func=AF.Copy),
                            r=[G_pf[bk]], w=[b_stV[sl]])

                def stage_B2(n):
                    si, gi, tt = steps[n]
                    c0, ncol, heads, sl = ginfo(si, gi)
                    if heads == "gates":
                        return
                    nK, nC, nV = hk(heads)
                    if nK == 0:
                        return
                    bk = 2 + (n % 4)
                    pt = pf[bk]
                    sq, kn, kb, hs, rp, b_sq, b_kn, b_kb, b_hs, b_rp = scr(n)
                    gain = gains[heads[0][1]]
                    pv = pt[:, 0:nK * 128].rearrange("p (h d) -> p h d", h=nK)
                    S.op("dve", lambda e: e.reciprocal(hs[:, 0:nK], hs[:, 4:4 + nK]), r=[b_hs], w=[b_hs])
                    S.op("dve", lambda e: e.tensor_tensor(out=kn[:, 0:nK, :], in0=pv,
                                                          in1=hs[:, 0:nK].unsqueeze(2).to_broadcast([128, nK, 128]), op=ALU.mult),
                         r=[G_pf[bk], b_hs], w=[b_kn])
                    S.op("dve", lambda e: e.tensor_tensor(out=kn[:, 0:nK, :], in0=kn[:, 0:nK, :],
                                                          in1=gain.unsqueeze(1).to_broadcast([128, nK, 128]), op=ALU.mult),
                         r=[b_kn, b_g], w=[b_kn])

                def stage_B3(n):
                    si, gi, tt = steps[n]
                    c0, ncol, heads, sl = ginfo(si, gi)
                    if heads == "gates":
                        return
                    nK, nC, nV = hk(heads)
                    if nK == 0:
                        return
                    sq, kn, kb, hs, rp, b_sq, b_kn, b_kb, b_hs, b_rp = scr(n)
                    ct_ = supers[si][1] + tt
                    cosb = cs[:, ct_, 0:16].unsqueeze(1).to_broadcast([128, nK, 16])
                    sinb = cs[:, ct_, 16:32].unsqueeze(1).to_broadcast([128, nK, 16])
                    x1 = kn[:, 0:nK, 0:16]
                    x2 = kn[:, 0:nK, 16:32]
                    S.op("pool", lambda e: e.tensor_tensor(out=rp[:, 0, 0:nK, :], in0=x1, in1=cosb, op=ALU.mult), r=[b_kn, b_cs], w=[b_rp])
                    S.op("pool", lambda e: e.tensor_tensor(out=rp[:, 1, 0:nK, :], in0=x2, in1=sinb, op=ALU.mult), r=[b_kn, b_cs], w=[b_rp])
                    S.op("pool", lambda e: e.tensor_tensor(out=rp[:, 2, 0:nK, :], in0=x2, in1=cosb, op=ALU.mult), r=[b_kn, b_cs], w=[b_rp])
                    S.op("pool", lambda e: e.tensor_tensor(out=rp[:, 3, 0:nK, :], in0=x1, in1=sinb, op=ALU.mult), r=[b_kn, b_cs], w=[b_rp])
                    S.op("pool", lambda e: e.tensor_tensor(out=kb[:, 0:nK, 0:16], in0=rp[:, 0, 0:nK, :], in1=rp[:, 1, 0:nK, :],
                                                           op=ALU.subtract), r=[b_rp], w=[b_kb])
                    S.op("pool", lambda e: e.tensor_tensor(out=kb[:, 0:nK, 16:32], in0=rp[:, 2, 0:nK, :], in1=rp[:, 3, 0:nK, :],
                                                           op=ALU.add), r=[b_rp], w=[b_kb])
                    S.op("act", lambda e: e.activation(out=kb[:, 0:nK, 32:128], in_=kn[:, 0:nK, 32:128], func=AF.Copy),
                         r=[b_kn], w=[b_kb])

                def stage_B4(n):
                    si, gi, tt = steps[n]
                    c0, ncol, heads, sl = ginfo(si, gi)
                    if heads == "gates":
                        if tt == ntile - 1:
                            S.dma("sp", GS[:, :, :], gsg[:], r=[b_gsg], w=[G_GS])
                        return
                    nK, nC, nV = hk(heads)
                    nTr = nK + nC
                    sq, kn, kb, hs, rp, b_sq, b_kn, b_kb, b_hs, b_rp = scr(n)
                    if nTr > 0:
                        ph = n % 2
                        for hq_ in range(nTr):
                            S.op("pe", lambda e, hq_=hq_: e.transpose(pb[ph][:, hq_ * 128:(hq_ + 1) * 128], kb[:, hq_, :], ident[:]),
                                 r=[b_kb, G_ident], w=[G_pb[ph]])
                        S.op("dve", lambda e: e.tensor_copy(stT[sl][:, 0:nTr, tt * 128:(tt + 1) * 128],
                                                            pb[ph][:, 0:nTr * 128].rearrange("p (h t) -> p h t", h=nTr)),
                             r=[G_pb[ph]], w=[b_stT[sl]])
                    if tt == ntile - 1:
                        hT_i = 0
                        hV_i = 0
                        for (kind, gk_, dst, idx, tok0) in heads:
                            if kind in ("K", "C"):
                                dt_, gb = (KT, G_KT) if dst == "KT" else (QT, G_QT)
                                S.dma("sp", dt_[idx, :, tok0:tok0 + ntile * 128], stT[sl][:, hT_i, :], r=[b_stT[sl]], w=[gb[idx]])
                                hT_i += 1
                            else:
                                S.dma("sp", VV[idx, tok0:tok0 + ntile * 128, :].rearrange("(t p) d -> p t d", p=128),
                                      stV[sl][:, :, hV_i, :], r=[b_stV[sl]], w=[G_VV[idx]])
                                hV_i += 1

                hTgen(0)
                hTgen(1)
                NSt = len(steps)
                for n in range(NSt + 4):
                    if 0 <= n - 4 < NSt:
                        stage_B4(n - 4)
                    if 0 <= n - 3 < NSt:
                        stage_B3(n - 3)
                    if 0 <= n - 2 < NSt:
                        stage_B2(n - 2)
                    if 0 <= n - 1 < NSt:
                        stage_B1(n - 1)
                    if n < NSt:
                        si, gi, tt = steps[n]
                        if si == 0 and gi == 0 and tt + 2 < ntile:
                            hTgen(tt + 2)
                        stage_A(n)
                return phase_done()

        if not stop:
            supers = []
            for st in range(4):
                tok0 = st * 1024
                groups = [
                    (1024, 512, [("C", None, "KT", 12, tok0), ("C", None, "KT", 13, tok0),
                                 ("C", None, "KT", 14, tok0), ("C", None, "KT", 15, tok0)]),
                    (1536, 512, [("K", "k1", "KT", 0, tok0), ("K", "k1", "KT", 1, tok0),
                                 ("V", None, "VV", 0, tok0), ("V", None, "VV", 1, tok0)]),
                    (2048, 512, [("K", "k2", "KT", 2, tok0), ("K", "k2", "KT", 3, tok0),
                                 ("V", None, "VV", 2, tok0), ("V", None, "VV", 3, tok0)]),
                    (3608, 512, [("K", "mk", "KT", 4 + h, tok0) for h in range(4)]),
                    (3608 + 512, 512, [("K", "mk", "KT", 8 + h, tok0) for h in range(4)]),
                    (4632, 512, [("V", None, "VV", 4 + h, tok0) for h in range(4)]),
                    (4632 + 512, 512, [("V", None, "VV", 8 + h, tok0) for h in range(4)]),
                ]
                supers.append((x_full[tok0:tok0 + 1024, :], st * 8, groups))
            groups = [
                (0, 512, [("K", "nq", "QT", h, 0) for h in range(4)]),
                (512, 512, [("K", "nq", "QT", 4 + h, 0) for h in range(4)]),
                (2560, 24, "gates"),
                (2584, 512, [("K", "mq", "QT", 8 + h, 0) for h in range(4)]),
                (2584 + 512, 512, [("K", "mq", "QT", 12 + h, 0) for h in range(4)]),
            ]
            supers.append((x_own[:, :], NT, groups))
            stop = run_proj_all(supers)

        def emit_normrope(src, n, gain, cosb, sinb, dst, sq, hs, kn, rp, bsrc, bdst, bs):
            b_sq, b_hs, b_kn, b_rp, b_g, b_cs = bs
            S.op("act", lambda e: e.activation(out=sq[:, 0:n * 128].rearrange("p (h d) -> p h d", h=n), in_=src,
                                               func=AF.Square), r=[bsrc], w=[b_sq])
            S.op("dve", lambda e: e.tensor_reduce(out=hs[:, 0:n], in_=sq[:, 0:n * 128].rearrange("p (h d) -> p h d", h=n),
                                                  axis=AX.X, op=ALU.add), r=[b_sq], w=[b_hs])
            S.op("act", lambda e: e.activation(out=hs[:, 4:4 + n], in_=hs[:, 0:n], func=AF.Sqrt, scale=1.0 / 128,
                                               bias=1e-6), r=[b_hs], w=[b_hs])
            S.op("dve", lambda e: e.reciprocal(hs[:, 0:n], hs[:, 4:4 + n]), r=[b_hs], w=[b_hs])
            S.op("dve", lambda e: e.tensor_tensor(out=kn[:, 0:n, :], in0=src,
                                                  in1=hs[:, 0:n].unsqueeze(2).to_broadcast([128, n, 128]), op=ALU.mult),
                 r=[bsrc, b_hs], w=[b_kn])
            S.op("dve", lambda e: e.tensor_tensor(out=kn[:, 0:n, :], in0=kn[:, 0:n, :], in1=gain, op=ALU.mult),
                 r=[b_kn, b_g], w=[b_kn])
            if cosb is None:
                S.op("act", lambda e: e.activation(out=dst, in_=kn[:, 0:n, :], func=AF.Copy), r=[b_kn], w=[bdst])
                return
            x1 = kn[:, 0:n, 0:16]
            x2 = kn[:, 0:n, 16:32]
            S.op("dve", lambda e: e.tensor_tensor(out=rp[:, 0, 0:n, :], in0=x1, in1=cosb, op=ALU.mult), r=[b_kn, b_cs], w=[b_rp])
            S.op("dve", lambda e: e.tensor_tensor(out=rp[:, 1, 0:n, :], in0=x2, in1=sinb, op=ALU.mult), r=[b_kn, b_cs], w=[b_rp])
            S.op("dve", lambda e: e.tensor_tensor(out=rp[:, 2, 0:n, :], in0=x2, in1=cosb, op=ALU.mult), r=[b_kn, b_cs], w=[b_rp])
            S.op("dve", lambda e: e.tensor_tensor(out=rp[:, 3, 0:n, :], in0=x1, in1=sinb, op=ALU.mult), r=[b_kn, b_cs], w=[b_rp])
            S.op("dve", lambda e: e.tensor_tensor(out=dst[:, :, 0:16], in0=rp[:, 0, 0:n, :], in1=rp[:, 1, 0:n, :],
                                                  op=ALU.subtract), r=[b_rp], w=[bdst])
            S.op("dve", lambda e: e.tensor_tensor(out=dst[:, :, 16:32], in0=rp[:, 2, 0:n, :], in1=rp[:, 3, 0:n, :],
                                                  op=ALU.add), r=[b_rp], w=[bdst])
            S.op("act", lambda e: e.activation(out=dst[:, :, 32:128], in_=kn[:, 0:n, 32:128], func=AF.Copy),
                 r=[b_kn], w=[bdst])

        kcT = E(sbt("kcT", [128, 2, 256], BF16))
        vca = E(sbt("vca", [128, 2, 2, 193], BF16))
        G_kcT, G_vca = S.buf("G_kcT"), S.buf("G_vca")
        if not stop:
            with ExitStack() as ps:
                P = ps.enter_context
                psum_alloc(P, 6, 2)
                w1s = [P(sbt("w1s%d" % i, [128, 32, 128], BF16)) for i in range(2)]
                w2s = [P(sbt("w2s%d" % i, [128, 128], BF16)) for i in range(2)]
                pe32 = [P(sbt("pe32%d" % i, [32, 128], F32)) for i in range(2)]
                id32 = P(sbt("id32", [32, 32], F32))
                peT = [P(sbt("peT%d" % i, [128, 32], F32)) for i in range(2)]
                kcm = P(sbt("kcm", [128, 256, 16], BF16))
                kp = P(sbt("kp", [128, 32, 255], BF16))
                hid = P(sbt("hid", [128, 256], BF16))
                cb = P(sbt("cb", [128, 2, 128], BF16))
                csC = P(sbt("csC", [128, 2, 32], F32))
                gk0 = P(sbt("gk0", [128, 128], F32))
                sq = P(sbt("sq2", [128, 512], F32))
                hs = P(sbt("hs2", [128, 8], F32))
                kn = P(sbt("kn2", [128, 4, 128], F32))
                rp = P(sbt("rp2", [128, 4, 4, 16], F32))
                b_w, b_pe, b_peT, b_kcm, b_kp, b_hid, b_cb = (S.buf("w"), S.buf("pe"), S.buf("peT"), S.buf("kcm"),
                                                               S.buf("kp"), S.buf("hid"), S.buf("cb"))
                bs = (S.buf("sq"), S.buf("hs"), S.buf("kn"), S.buf("rp"), S.buf("g"), S.buf("cs"))
                S.op("pool", lambda e: e.memset(hid[:], 0.0), w=[b_hid])
                for g in range(2):
                    S.dma("sp", vca[:, g, :, 128:193], ovaug_d[:, :, :], w=[G_vca])
                S.dma("sp", csC[:], cs_cmp[:, :, :], w=[bs[5]])
                S.dma("sp", gk0[:], g_nk[:, 0, :], w=[bs[4]])
                S.dma("sp", id32[:], ident32_d[:, :], w=[b_pe])
                for kv, (w1d, w2d, ped) in enumerate(((w1_k, w2_k, pe_k), (w1_v, w2_v, pe_v))):
                    S.dma("pool", w1s[kv][:], w1d.rearrange("(p d) o -> d p o", d=128), w=[b_w])
                    S.dma("pool", w2s[kv][:], w2d[:, :], w=[b_w])
                    S.dma("sp", pe32[kv][:], ped[:, :], w=[b_pe])
                    S.op("pe", lambda e, kv=kv: e.matmul(pf[0][:, 0:32], lhsT=pe32[kv][:], rhs=id32[:], start=True, stop=True),
                         r=[b_pe], w=[G_pf[0]])
                    S.op("dve", lambda e, kv=kv: e.tensor_copy(peT[kv][:], pf[0][:, 0:32]), r=[G_pf[0]], w=[b_peT])
                for kv in range(2):
                    for g in range(2):
                        S.dma("sp", kcm[:].rearrange("p a b -> p (a b)"), KT[12 + 2 * kv + g, :, :], r=[G_KT[12 + 2 * kv + g]],
                              w=[b_kcm])
                        S.op("dve", lambda e, kv=kv: e.tensor_tensor(
                            out=kp[:, 0:16, :], in0=kcm[:, 0:255, :].rearrange("d c p -> d p c"),
                            in1=peT[kv][:, 0:16].unsqueeze(2).to_broadcast([128, 16, 255]), op=ALU.add),
                            r=[b_kcm, b_peT], w=[b_kp])
                        S.op("dve", lambda e, kv=kv: e.tensor_tensor(
                            out=kp[:, 16:32, :], in0=kcm[:, 1:256, :].rearrange("d c p -> d p c"),
                            in1=peT[kv][:, 16:32].unsqueeze(2).to_broadcast([128, 16, 255]), op=ALU.add),
                            r=[b_kcm, b_peT], w=[b_kp])
                        for p in range(32):
                            S.op("pe", lambda e, kv=kv, p=p: e.matmul(pf[1][:, 0:255], lhsT=w1s[kv][:, p, :], rhs=kp[:, p, :],
                                                                      start=(p == 0), stop=(p == 31)),
                                 r=[b_w, b_kp], w=[G_pf[1]])
                        S.op("act", lambda e: e.activation(out=hid[:, 0:255], in_=pf[1][:, 0:255], func=AF.Silu),
                             r=[G_pf[1]], w=[b_hid])
                        for ct in range(2):
                            S.op("pe", lambda e, kv=kv, ct=ct: e.matmul(pf[0][:, ct * 128:(ct + 1) * 128],
                                                                        lhsT=hid[:, ct * 128:(ct + 1) * 128], rhs=w2s[kv][:],
                                                                        start=True, stop=True),
                                 r=[b_w, b_hid], w=[G_pf[0]])
                        src = pf[0][:, 0:256].rearrange("p (h d) -> p h d", h=2)
                        if kv == 1:
                            S.op("act", lambda e, g=g, src=src: e.activation(out=vca[:, g, :, 0:128], in_=src, func=AF.Copy),
                                 r=[G_pf[0]], w=[G_vca])
                        else:
                            emit_normrope(src, 2, gk0[:].unsqueeze(1).to_broadcast([128, 2, 128]), csC[:, :, 0:16],
                                          csC[:, :, 16:32], cb[:], sq, hs, kn, rp, G_pf[0], b_cb, bs)
                            for ct in range(2):
                                S.op("pe", lambda e, ct=ct: e.transpose(pb[0][:, ct * 128:(ct + 1) * 128], cb[:, ct, :], ident[:]),
                                     r=[b_cb, G_ident], w=[G_pb[0]])
                            S.op("dve", lambda e, g=g: e.tensor_copy(kcT[:, g, :], pb[0][:, 0:256]), r=[G_pb[0]], w=[G_kcT])
                stop = phase_done()

        def Ov(t, w):
            return t[:, 0:2 * w].rearrange("p (a b) -> p a b", a=2)

        def head_out(src4, gsl, ob, sq, hs, kn, bsrc, b_ob, bs, ot_idx0, i, stg, b_stg):
            b_sq, b_hs, b_kn, b_rp, b_g, b_cs = bs
            sqv = sq[:, 0:512].rearrange("p (h d) -> p h d", h=4)
            S.op("dve", lambda e: e.tensor_tensor(out=sqv, in0=src4, in1=src4, op=ALU.mult), r=[bsrc], w=[b_sq])
            S.op("dve", lambda e: e.tensor_reduce(out=hs[:, 0:4], in_=sqv, axis=AX.X, op=ALU.add), r=[b_sq], w=[b_hs])
            S.op("act", lambda e: e.activation(out=hs[:, 4:8], in_=hs[:, 0:4], func=AF.Ln, scale=1.0 / 128, bias=1e-6),
                 r=[b_hs], w=[b_hs])
            S.op("act", lambda e: e.activation(out=hs[:, 0:4], in_=hs[:, 4:8], func=AF.Exp, scale=-0.5), r=[b_hs], w=[b_hs])
            S.op("dve", lambda e: e.tensor_tensor(out=kn[:, 0:4, :], in0=src4,
                                                  in1=hs[:, 0:4].unsqueeze(2).to_broadcast([128, 4, 128]), op=ALU.mult),
                 r=[bsrc, b_hs], w=[b_kn])
            S.op("dve", lambda e: e.tensor_tensor(out=ob[:], in0=kn[:, 0:4, :], in1=gsl, op=ALU.mult),
                 r=[b_kn, b_g], w=[b_ob])
            for r in range(4):
                S.op("pe", lambda e, r=r: e.transpose(pb[0][:, 512 + r * 128:512 + (r + 1) * 128], ob[:, r, :], ident[:]),
                     r=[b_ob, G_ident], w=[G_pbh[0]])
            S.op("dve", lambda e: e.tensor_copy(stg[:], pb[0][:, 512:1024].rearrange("p (h t) -> p h t", h=4)),
                 r=[G_pbh[0]], w=[b_stg])
            S.dma("sp", OT[ot_idx0:ot_idx0 + 4, :, i * 128:(i + 1) * 128].rearrange("h d t -> d h t"), stg[:],
                  r=[b_stg], w=[G_OT[ot_idx0 + k] for k in range(4)])

        def head_tail(pre, src4, gsl, ob, sq, hs, kn, bsrc, b_ob, bs, ot_idx0, i, stg, b_stg):
            b_sq, b_hs, b_kn, b_rp, b_g, b_cs = bs
            sqv = sq[:, 0:512].rearrange("p (h d) -> p h d", h=4)

            def a1():
                pre()
                S.op("dve", lambda e: e.tensor_tensor(out=sqv, in0=src4, in1=src4, op=ALU.mult), r=[bsrc], w=[b_sq])
                S.op("dve", lambda e: e.tensor_reduce(out=hs[:, 0:4], in_=sqv, axis=AX.X, op=ALU.add), r=[b_sq], w=[b_hs])

            def a2():
                S.op("act", lambda e: e.activation(out=hs[:, 4:8], in_=hs[:, 0:4], func=AF.Ln, scale=1.0 / 128, bias=1e-6),
                     r=[b_hs], w=[b_hs])
                S.op("act", lambda e: e.activation(out=hs[:, 0:4], in_=hs[:, 4:8], func=AF.Exp, scale=-0.5), r=[b_hs], w=[b_hs])

            def a3():
                S.op("dve", lambda e: e.tensor_tensor(out=kn[:, 0:4, :], in0=src4,
                                                      in1=hs[:, 0:4].unsqueeze(2).to_broadcast([128, 4, 128]), op=ALU.mult),
                     r=[bsrc, b_hs], w=[b_kn])
                S.op("dve", lambda e: e.tensor_tensor(out=ob[:], in0=kn[:, 0:4, :], in1=gsl, op=ALU.mult),
                     r=[b_kn, b_g], w=[b_ob])

            def b_():
                for r in range(4):
                    S.op("pe", lambda e, r=r: e.transpose(pb[0][:, 512 + r * 128:512 + (r + 1) * 128], ob[:, r, :], ident[:]),
                         r=[b_ob, G_ident], w=[G_pbh[0]])
                S.op("dve", lambda e: e.tensor_copy(stg[:], pb[0][:, 512:1024].rearrange("p (h t) -> p h t", h=4)),
                     r=[G_pbh[0]], w=[b_stg])
                S.dma("sp", OT[ot_idx0:ot_idx0 + 4, :, i * 128:(i + 1) * 128].rearrange("h d t -> d h t"), stg[:],
                      r=[b_stg], w=[G_OT[ot_idx0 + k] for k in range(4)])

            return [a1, a2, a3, b_]

        if not stop:
            with ExitStack() as ps:
                P = ps.enter_context
                psum_alloc(P, 7, 1)
                ksT = P(sbt("ksT", [128, T], BF16))
                kwT = P(sbt("kwT", [128, T], BF16))
                vs = P(sbt("vs", [128, NT, 129], BF16))
                vw = P(sbt("vw", [128, NT, 129], BF16))
                qT = [P(sbt("qT%d" % k, [128, 4, 128], BF16)) for k in range(2)]
                cmpm = P(sbt("cmpm", [128, NO, 2, 128], BF16))
                cmk = P(sbt("cmk", [128, 4, 128], BF16))
                wmk = P(sbt("wmk", [128, 8, 128], BF16))
                adds = P(sbt("adds", [128, NO, 64], F32))
                xsel = P(sbt("xsel", [64, T], BF16))
                gs = P(sbt("gs", [128, NO, 24], F32))
                gout = P(sbt("gout", [128, 1024], F32))
                Ec = P(sbt("Ec", [128, 2, 512], BF16))
                Et = [P(sbt("Et%d" % k, [128, 512], BF16)) for k in range(3)]
                Oc = P(sbt("Oc", [128, 4, 193], F32))
                Os = P(sbt("Os", [128, 4, 129], F32))
                Ow = P(sbt("Ow", [128, 4, 129], F32))
                rinv = P(sbt("rinv", [128, 4, 3], F32))
                coef = P(sbt("coef", [128, 4, 3], F32))
                imp = P(sbt("imp", [128, 64], F32))
                sc2 = P(sbt("sc2s", [128, 64], F32))
                m8 = P(sbt("m8", [128, 16], F32))
                negb = P(sbt("negb", [128, 64], BF16))
                negT = P(sbt("negT", [64, 128], BF16))
                o4 = P(sbt("o4", [128, 4, 128], F32))
                ob = P(sbt("ob", [128, 4, 128], BF16))
                stg = P(sbt("stg", [128, 4, 128], BF16))
                sq = P(sbt("sq3", [128, 512], F32))
                hs = P(sbt("hs3", [128, 8], F32))
                kn = P(sbt("kn3", [128, 4, 128], F32))
                b_kv, b_q, b_c = S.buf("kv"), S.bufs_n("q", 2), S.buf("consts")
                b_Ec, b_Et, b_Oc, b_Os, b_Ow = S.buf("Ec"), S.bufs_n("Et", 3), S.buf("Oc"), S.buf("Os"), S.buf("Ow")
                b_ri, b_co, b_imp, b_sc2, b_m8, b_negb, b_negT = (S.buf("ri"), S.buf("co"), S.buf("imp"), S.buf("sc2"),
                                                                   S.buf("m8"), S.buf("negb"), S.buf("negT"))
                b_o4, b_ob, b_stg = S.buf("o4"), S.buf("ob"), S.buf("stg")
                bs = (S.buf("sq"), S.buf("hs"), S.buf("kn"), S.buf("rp"), b_c, b_c)
                S.dma("sp", cmpm[:], cmpmask_d[:, :, :, :], w=[b_c])
                S.dma("sp", cmk[:], cmask_d[:, :, :], w=[b_c])
                S.dma("sp", wmk[:], wmask_d[:, :, :], w=[b_c])
                S.dma("sp", adds[:], addsel_d[:, :, :], w=[b_c])
                S.dma("sp", xsel[:], xsel_d[:, :], w=[b_c])
                S.dma("sp", gs[:], GS[:, :, :], r=[G_GS], w=[b_c])
                S.dma("sp", gout[:], g_out[:, 0:1024], w=[b_c])
                S.op("pool", lambda e: e.memset(vs[:, :, 128:129], 1.0), w=[b_kv])
                S.op("pool", lambda e: e.memset(vw[:, :, 128:129], 1.0), w=[b_kv])
                qn = 0
                for g in range(2):
                    S.dma("sp", ksT[:], KT[g, :, :], r=[G_KT[g]], w=[b_kv])
                    S.dma("sp", kwT[:], KT[2 + g, :, :], r=[G_KT[2 + g]], w=[b_kv])
                    S.dma("sp", vs[:, :, 0:128], VV[g, :, :].rearrange("(t p) d -> p t d", p=128), r=[G_VV[g]], w=[b_kv])
                    S.dma("sp", vw[:, :, 0:128], VV[2 + g, :, :].rearrange("(t p) d -> p t d", p=128), r=[G_VV[2 + g]],
                          w=[b_kv])
                    def q_load(i_):
                        S.dma("sp", qT[i_ % 2][:], QT[4 * g:4 * g + 4, :, i_ * 128:(i_ + 1) * 128].rearrange("h d t -> d h t"),
                              r=[G_QT[4 * g + k] for k in range(4)], w=[b_q[i_ % 2]])

                    q_load(0)
                    for i in range(NO):
                        q = qT[i % 2]
                        bq = b_q[i % 2]
                        if i + 1 < NO:
                            q_load(i + 1)
                        qv = q[:].rearrange("p h t -> p (h t)")
                        for ct in range(2):
                            S.op("pe", lambda e, ct=ct, g=g, qv=qv: e.matmul(pf[ct][:], lhsT=kcT[:, g, ct * 128:(ct + 1) * 128],
                                                                            rhs=qv, start=True, stop=False),
                                 r=[G_kcT, bq], w=[G_pf[ct]])
                            S.op("pe", lambda e, ct=ct, i=i: e.matmul(
                                pf[ct][:], lhsT=ident[:], rhs=cmpm[:, i, ct, :].unsqueeze(1).to_broadcast([128, 4, 128]),
                                start=False, stop=True), r=[G_ident, b_c], w=[G_pf[ct]])
                            S.op("act", lambda e, ct=ct: e.activation(out=Ec[:, ct, :], in_=pf[ct][:], func=AF.Exp, scale=SCALE),
                                 r=[G_pf[ct]], w=[b_Ec])
                        for r in range(4):
                            for ct in range(2):
                                S.op("pe", lambda e, r=r, ct=ct, g=g: e.matmul(
                                    Ov(pf[5 + r // 2], 193)[:, r % 2, :], lhsT=Ec[:, ct, r * 128:(r + 1) * 128],
                                    rhs=vca[:, g, ct, :], start=(ct == 0), stop=(ct == 1)),
                                    r=[b_Ec, G_vca], w=[G_pf[5 + r // 2]])
                        for hh in range(2):
                            S.op("dve", lambda e, hh=hh: e.tensor_copy(Oc[:, 2 * hh:2 * hh + 2, :], Ov(pf[5 + hh], 193)),
                                 r=[G_pf[5 + hh]], w=[b_Oc])
                        S.op("dve", lambda e: e.tensor_scalar(rinv[:, :, 0], Oc[:, :, 128], 1e-30, None, op0=ALU.add),
                             r=[b_Oc], w=[b_ri])
                        S.op("dve", lambda e: e.reciprocal(rinv[:, :, 0], rinv[:, :, 0]), r=[b_ri], w=[b_ri])
                        S.op("dve", lambda e: e.tensor_scalar(imp[:], Oc[:, 0, 129:193], rinv[:, 0, 0:1], None, op0=ALU.mult),
                             r=[b_Oc, b_ri], w=[b_imp])
                        for r in range(1, 4):
                            S.op("dve", lambda e, r=r: e.scalar_tensor_tensor(out=imp[:], in0=Oc[:, r, 129:193],
                                                                              scalar=rinv[:, r, 0:1], in1=imp[:],
                                                                              op0=ALU.mult, op1=ALU.add),
                                 r=[b_Oc, b_ri, b_imp], w=[b_imp])
                        S.op("dve", lambda e, i=i: e.tensor_tensor(out=imp[:], in0=imp[:], in1=adds[:, i, :], op=ALU.add),
                             r=[b_imp, b_c], w=[b_imp])
                        S.op("dve", lambda e: e.max(out=m8[:, 0:8], in_=imp[:]), r=[b_imp], w=[b_m8])
                        S.op("dve", lambda e: e.match_replace(out=sc2[:], in_to_replace=m8[:, 0:8], in_values=imp[:],
                                                              imm_value=-1e30), r=[b_imp, b_m8], w=[b_sc2])
                        S.op("dve", lambda e: e.max(out=m8[:, 8:16], in_=sc2[:]), r=[b_sc2], w=[b_m8])
                        S.op("dve", lambda e: e.tensor_scalar(sc2[:], imp[:], m8[:, 15:16], None, op0=ALU.is_ge),
                             r=[b_imp, b_m8], w=[b_sc2])
                        S.op("dve", lambda e: e.tensor_scalar(negb[:], sc2[:], -1.0, -NEG, op0=ALU.add, op1=ALU.mult),
                             r=[b_sc2], w=[b_negb])
                        stp = []
                        kts1 = [4 * i - 4 + w for w in range(8) if 4 * i - 4 + w >= 0]
                        for n_, kt in enumerate(kts1):
                            stp.append((1, kt, n_, n_ == len(kts1) - 1))
                        kts0 = list(range(4 * i + 4))
                        for n_, kt in enumerate(kts0):
                            stp.append((0, kt, n_, n_ == len(kts0) - 1))
                        first_slc = len(kts1)

                        def st_S(n):
                            br_, kt, n_, last = stp[n]
                            sl = n % 3
                            pt = pf[sl]
                            ii, qq = i, qv
                            if n == first_slc:
                                S.op("pe", lambda e: e.transpose(pb[0][0:64, 0:128], negb[:], ident[:]), r=[b_negb, G_ident],
                                     w=[G_pb[0]])
                                S.op("dve", lambda e: e.tensor_copy(negT[:], pb[0][0:64, 0:128]), r=[G_pb[0]], w=[b_negT])
                            Ksrc = ksT if br_ == 0 else kwT
                            S.op("pe", lambda e: e.matmul(pt[:], lhsT=Ksrc[:, kt * 128:(kt + 1) * 128], rhs=qq, start=True, stop=False),
                                 r=[b_kv, bq], w=[G_pf[sl]])
                            if br_ == 0:
                                S.op("pe", lambda e: e.matmul(
                                    pt[:], lhsT=xsel[:, kt * 128:(kt + 1) * 128],
                                    rhs=negT[:].unsqueeze(1).to_broadcast([64, 4, 128]), start=False, stop=(kt < 4 * ii)),
                                    r=[b_c, b_negT], w=[G_pf[sl]])
                                if kt >= 4 * ii:
                                    S.op("pe", lambda e: e.matmul(
                                        pt[:], lhsT=ident[:], rhs=cmk[:, kt - 4 * ii, :].unsqueeze(1).to_broadcast([128, 4, 128]),
                                        start=False, stop=True), r=[G_ident, b_c], w=[G_pf[sl]])
                            else:
                                w_ = kt - (4 * ii - 4)
                                S.op("pe", lambda e: e.matmul(
                                    pt[:], lhsT=ident[:], rhs=wmk[:, w_, :].unsqueeze(1).to_broadcast([128, 4, 128]),
                                    start=False, stop=True), r=[G_ident, b_c], w=[G_pf[sl]])
                            S.op("act", lambda e: e.activation(out=Et[sl][:], in_=pt[:], func=AF.Exp, scale=SCALE),
                                 r=[G_pf[sl]], w=[b_Et[sl]])

                        def st_PV(n):
                            br_, kt, n_, last = stp[n]
                            sl = n % 3
                            Vsrc = vs if br_ == 0 else vw
                            ob_ = 5 if br_ == 0 else 3
                            for r in range(4):
                                S.op("pe", lambda e, r=r: e.matmul(
                                    Ov(pf[ob_ + r // 2], 129)[:, r % 2, :], lhsT=Et[sl][:, r * 128:(r + 1) * 128],
                                    rhs=Vsrc[:, kt, :], start=(n_ == 0 and r % 2 == 0), stop=(last and r % 2 == 1)),
                                    r=[b_Et[sl], b_kv], w=[G_pf[ob_ + r // 2]])
                            if last:
                                Od, b_Od = (Os, b_Os) if br_ == 0 else (Ow, b_Ow)
                                for hh in range(2):
                                    S.op("dve", lambda e, hh=hh: e.tensor_copy(Od[:, 2 * hh:2 * hh + 2, :], Ov(pf[ob_ + hh], 129)),
                                         r=[G_pf[ob_ + hh]], w=[b_Od])
                                S.op("dve", lambda e: e.tensor_scalar(rinv[:, :, 1 + br_], Od[:, :, 128], 1e-30, None, op0=ALU.add),
                                     r=[b_Od], w=[b_ri])
                                S.op("dve", lambda e: e.reciprocal(rinv[:, :, 1 + br_], rinv[:, :, 1 + br_]), r=[b_ri], w=[b_ri])

                        st_S(0)
                        if len(stp) > 1:
                            st_S(1)
                        for n in range(len(stp)):
                            if n + 2 < len(stp):
                                st_S(n + 2)
                            st_PV(n)
                        S.op("dve", lambda e, g=g, i=i: e.tensor_tensor(
                            out=coef[:], in0=rinv[:], in1=gs[:, i, g * 12:(g + 1) * 12].rearrange("p (r x) -> p r x", r=4),
                            op=ALU.mult), r=[b_ri, b_c], w=[b_co])
                        for r in range(4):
                            S.op("dve", lambda e, r=r: e.tensor_scalar(o4[:, r, :], Oc[:, r, 0:128], coef[:, r, 0:1], None,
                                                                       op0=ALU.mult), r=[b_Oc, b_co], w=[b_o4])
                            S.op("dve", lambda e, r=r: e.scalar_tensor_tensor(out=o4[:, r, :], in0=Os[:, r, 0:128],
                                                                              scalar=coef[:, r, 1:2], in1=o4[:, r, :],
                                                                              op0=ALU.mult, op1=ALU.add),
                                 r=[b_Os, b_co, b_o4], w=[b_o4])
                            S.op("dve", lambda e, r=r: e.scalar_tensor_tensor(out=o4[:, r, :], in0=Ow[:, r, 0:128],
                                                                              scalar=coef[:, r, 2:3], in1=o4[:, r, :],
                                                                              op0=ALU.mult, op1=ALU.add),
                                 r=[b_Ow, b_co, b_o4], w=[b_o4])
                        head_out(o4[:], gout[:, g * 512:(g + 1) * 512].rearrange("p (h d) -> p h d", h=4), ob, sq, hs, kn,
                                 b_o4, b_ob, bs, 4 * g, i, stg, b_stg)
                stop = phase_done()

        for hq in range(2):
            if stop:
                break
            with ExitStack() as ps:
                P = ps.enter_context
                psum_alloc(P, 7, 1)
                kT4 = P(sbt("kT4", [128, 4, T], BF16))
                v4 = P(sbt("v4", [128, 4, NT, 129], BF16))
                qT = [P(sbt("mqT%d" % k, [128, 4, 128], BF16)) for k in range(2)]
                cmk = P(sbt("mcmk", [128, 4, 128], BF16))
                addm = P(sbt("addm", [128, NO, 16], F32))
                xmo = P(sbt("xmo", [16, T], BF16))
                gout = P(sbt("mgout", [128, 512], F32))
                km = P(sbt("km", [128, 64], F32))
                kmb = P(sbt("kmb", [128, 4, 16], BF16))
                Et = [P(sbt("mEt%d" % k, [128, 512], BF16)) for k in range(3)]
                Om = P(sbt("Om", [128, 4, 129], F32))
                rinv = P(sbt("mrinv", [128, 4], F32))
                scm = P(sbt("scm", [128, 4, 16], F32))
                selm = P(sbt("selm", [128, 4, 16], F32))
                m8 = P(sbt("mm8", [128, 4, 8], F32))
                negb = P(sbt("mnegb", [128, 4, 16], BF16))
                negT = P(sbt("mnegT", [16, 512], BF16))
                o4 = P(sbt("mo4", [128, 4, 128], F32))
                ob = P(sbt("mob", [128, 4, 128], BF16))
                stg = P(sbt("mstg", [128, 4, 128], BF16))
                sq = P(sbt("sq4", [128, 512], F32))
                hs = P(sbt("hs4", [128, 8], F32))
                kn = P(sbt("kn4", [128, 4, 128], F32))
                b_kv, b_q, b_c = S.buf("kv"), S.bufs_n("q", 2), S.buf("consts")
                b_km, b_Et, b_Om, b_ri, b_scm, b_selm, b_m8, b_negb, b_negT = (
                    S.buf("km"), S.bufs_n("Et", 3), S.buf("Om"), S.buf("ri"), S.buf("scm"), S.buf("selm"), S.buf("m8"),
                    S.buf("negb"), S.buf("negT"))
                b_o4, b_ob, b_stg = S.buf("o4"), S.buf("ob"), S.buf("stg")
                bs = (S.buf("sq"), S.buf("hs"), S.buf("kn"), S.buf("rp"), b_c, b_c)
                S.dma("sp", cmk[:], cmask_d[:, :, :], w=[b_c])
                S.dma("sp", addm[:], addmoba_d[:, :, :], w=[b_c])
                S.dma("sp", xmo[:], xmoba_d[:, :], w=[b_c])
                S.dma("sp", gout[:], g_out[:, 1024 + hq * 512:1024 + (hq + 1) * 512], w=[b_c])
                S.op("pool", lambda e: e.memset(v4[:, :, :, 128:129], 1.0), w=[b_kv])
                for h in range(4):
                    S.dma("sp", kT4[:, h, :], KT[4 + 4 * hq + h, :, :], r=[G_KT[4 + 4 * hq + h]], w=[b_kv])
                    S.dma("sp", v4[:, h, :, 0:128], VV[4 + 4 * hq + h, :, :].rearrange("(t p) d -> p t d", p=128),
                          r=[G_VV[4 + 4 * hq + h]], w=[b_kv])
                S.op("dve", lambda e: e.tensor_reduce(out=km[:], in_=kT4[:].rearrange("p h (n k) -> p (h n) k", k=256),
                                                      axis=AX.X, op=ALU.add), r=[b_kv], w=[b_km])
                S.op("dve", lambda e: e.tensor_scalar(kmb[:].rearrange("p h n -> p (h n)"), km[:], 1.0 / 256, None, op0=ALU.mult),
                     r=[b_km], w=[b_km])
                def mq_load(i_):
                    S.dma("sp", qT[i_ % 2][:], QT[8 + 4 * hq:12 + 4 * hq, :, i_ * 128:(i_ + 1) * 128].rearrange("h d t -> d h t"),
                          r=[G_QT[8 + 4 * hq + k] for k in range(4)], w=[b_q[i_ % 2]])

                mq_load(0)
                pending = []
                ada_load, ada_compute = ada_setup(P)
                ada_cgs = list(range(8 + 8 * hq, 16 + 8 * hq))
                ada_load(ada_cgs[0])
                for i in range(NO):
                    q = qT[i % 2]
                    bq = b_q[i % 2]
                    if i + 1 < NO:
                        mq_load(i + 1)
                        ada_load(ada_cgs[i + 1])
                    ada_compute(ada_cgs[i], 6)
                    for h in range(4):
                        S.op("pe", lambda e, h=h, q=q: e.matmul(pf[5][:, h * 16:(h + 1) * 16], lhsT=q[:, h, :], rhs=kmb[:, h, :],
                                                               start=(h == 0), stop=(h == 3)), r=[bq, b_km], w=[G_pf[5]])
                    S.op("dve", lambda e, i=i: e.tensor_tensor(
                        out=scm[:], in0=pf[5][:, 0:64].rearrange("p (h n) -> p h n", h=4),
                        in1=addm[:, i, :].unsqueeze(1).to_broadcast([128, 4, 16]), op=ALU.add), r=[G_pf[5], b_c], w=[b_scm])
                    for h in range(4):
                        S.op("dve", lambda e, h=h: e.max(out=m8[:, h, :], in_=scm[:, h, :]), r=[b_scm], w=[b_m8])
                    S.op("dve", lambda e: e.tensor_tensor(out=selm[:], in0=scm[:], in1=m8[:, :, 3:4].to_broadcast([128, 4, 16]),
                                                          op=ALU.is_ge), r=[b_scm, b_m8], w=[b_selm])
                    S.op("dve", lambda e: e.tensor_scalar(negb[:], selm[:], -1.0, -NEG, op0=ALU.add, op1=ALU.mult),
                         r=[b_selm], w=[b_negb])
                    for h in range(4):
                        S.op("pe", lambda e, h=h: e.transpose(pb[0][0:16, h * 128:(h + 1) * 128], negb[:, h, :], ident[:]),
                             r=[b_negb, G_ident], w=[G_pb[0]])
                    S.op("dve", lambda e: e.tensor_copy(negT[:], pb[0][0:16, 0:512]), r=[G_pb[0]], w=[b_negT])
                    nk = 4 * i + 4

                    def mo_S(kt):
                        sl = kt % 3
                        pt = pf[sl]
                        ii, qq = i, q
                        for h in range(4):
                            S.op("pe", lambda e, h=h: e.matmul(
                                pt[:, h * 128:(h + 1) * 128], lhsT=kT4[:, h, kt * 128:(kt + 1) * 128], rhs=qq[:, h, :],
                                start=(h == 0), stop=False), r=[b_kv, bq], w=[G_pf[sl]])
                        S.op("pe", lambda e: e.matmul(pt[:], lhsT=xmo[:, kt * 128:(kt + 1) * 128], rhs=negT[:],
                                                      start=False, stop=(kt < 4 * ii)),
                             r=[b_c, b_negT], w=[G_pf[sl]])
                        if kt >= 4 * ii:
                            S.op("pe", lambda e: e.matmul(
                                pt[:], lhsT=ident[:], rhs=cmk[:, kt - 4 * ii, :].unsqueeze(1).to_broadcast([128, 4, 128]),
                                start=False, stop=True), r=[G_ident, b_c], w=[G_pf[sl]])
                        S.op("act", lambda e: e.activation(out=Et[sl][:], in_=pt[:], func=AF.Exp, scale=SCALE),
                             r=[G_pf[sl]], w=[b_Et[sl]])

                    def mo_PV(kt):
                        sl = kt % 3
                        nk_ = nk
                        for h in range(4):
                            S.op("pe", lambda e, h=h: e.matmul(
                                Ov(pf[3 + h // 2], 129)[:, h % 2, :], lhsT=Et[sl][:, h * 128:(h + 1) * 128], rhs=v4[:, h, kt, :],
                                start=(kt == 0 and h % 2 == 0), stop=(kt == nk_ - 1 and h % 2 == 1)),
                                r=[b_Et[sl], b_kv], w=[G_pf[3 + h // 2]])

                    if pending:
                        pending[0]()
                    mo_S(0)
                    if nk > 1:
                        mo_S(1)
                    for kt in range(nk):
                        if kt + 2 < nk:
                            mo_S(kt + 2)
                        mo_PV(kt)
                        if pending and kt in (3, 4, 6):
                            pending[{3: 1, 4: 2, 6: 3}[kt]]()
                    pending = []
                    for hh in range(2):
                        S.op("dve", lambda e, hh=hh: e.tensor_copy(Om[:, 2 * hh:2 * hh + 2, :], Ov(pf[3 + hh], 129)),
                             r=[G_pf[3 + hh]], w=[b_Om])

                    def mo_pre():
                        S.op("dve", lambda e: e.tensor_scalar(rinv[:], Om[:, :, 128], 1e-30, None, op0=ALU.add), r=[b_Om], w=[b_ri])
                        S.op("dve", lambda e: e.reciprocal(rinv[:], rinv[:]), r=[b_ri], w=[b_ri])
                        S.op("dve", lambda e: e.tensor_tensor(out=o4[:], in0=Om[:, :, 0:128],
                                                              in1=rinv[:].unsqueeze(2).to_broadcast([128, 4, 128]), op=ALU.mult),
                             r=[b_Om, b_ri], w=[b_o4])

                    pending = head_tail(mo_pre, o4[:], gout[:].rearrange("p (h d) -> p h d", h=4), ob, sq, hs, kn, b_o4, b_ob,
                                        bs, 8 + 4 * hq, i, stg, b_stg)
                for f_ in pending:
                    f_()
                stop = phase_done()

        if not stop:
            with ExitStack() as ps:
                P = ps.enter_context
                psum_alloc(P, 6, 2)
                oTb = P(sbt("oTb", [128, 16, 512], BF16))
                rows = [P(sbt("rows%d" % k, [128, D], F32)) for k in range(2)]
                x1 = P(sbt("x1", [128, 4, D], F32))
                wbuf = [P(sbt("wbuf%d" % k, [128, 16, 256], BF16)) for k in range(4)]
                hT2 = P(sbt("hT2", [128, 16, 512], BF16))
                actT = P(sbt("actT", [128, NHC, 512], BF16))
                t1 = P(sbt("t1f", [128, D], F32))
                hb = P(sbt("hbf", [128, D], BF16))
                wfo = [P(sbt("wfo%d" % k, [128, 11, 256], BF16)) for k in range(2)]
                ss = P(sbt("ssf", [128, 4], F32))
                sg = P(sbt("sg", [128, 512], F32))
                b_oTb, b_rows, b_x1, b_wbuf, b_hT2, b_actT, b_t1, b_hb, b_wfo, b_ss, b_sg = (
                    S.buf("oTb"), S.bufs_n("rows", 2), S.bufs_n("x1_", 4), S.bufs_n("wbuf", 4), S.buf("hT2"), S.buf("actT"),
                    S.buf("t1"), S.buf("hb"), S.bufs_n("wfo", 2), S.buf("ss"), S.buf("sg"))
                for tb in range(2):
                    S.dma("sp", oTb[:], OT[:, :, tb * 512:(tb + 1) * 512].rearrange("h d t -> d h t"), r=G_OT, w=[b_oTb])
                    S.dma("sp", rows[0][:], modscr[:, 2 * D:3 * D], r=[G_mod], w=[b_rows[0]])
                    for tt in range(4):
                        S.dma("sp", x1[:, tt, :], x_own[(tb * 4 + tt) * 128:(tb * 4 + tt + 1) * 128, :], w=[b_x1[tt]])
                    w_out_v = w_out.rearrange("(h p) n -> p h n", p=128)
                    wn = 0
                    S.dma("pool", wbuf[0][:], w_out_v[:, :, 0:256], w=[b_wbuf[0]])
                    for cg in range(8):
                        sl = cg % 2
                        if cg + 1 < 8:
                            S.dma("pool", wbuf[1 - sl][:], w_out_v[:, :, (cg + 1) * 256:(cg + 2) * 256], w=[b_wbuf[1 - sl]])
                        for tt in range(4):
                            bk = 4 + tt % 2
                            for h in range(16):
                                S.op("pe", lambda e, bk=bk, h=h, tt=tt, sl=sl: e.matmul(
                                    pf[bk][:, 0:256], lhsT=oTb[:, h, tt * 128:(tt + 1) * 128], rhs=wbuf[sl][:, h, :],
                                    start=(h == 0), stop=(h == 15)), r=[b_oTb, b_wbuf[sl]], w=[G_pf[bk]])
                            S.op("dve", lambda e, bk=bk, cg=cg: e.tensor_tensor(out=t1[:, 0:256], in0=pf[bk][:, 0:256],
                                                                               in1=rows[0][:, cg * 256:(cg + 1) * 256], op=ALU.mult),
                                 r=[G_pf[bk], b_rows[0]], w=[b_t1])
                            S.op("dve", lambda e, tt=tt, cg=cg: e.tensor_tensor(out=x1[:, tt, cg * 256:(cg + 1) * 256],
                                                                               in0=x1[:, tt, cg * 256:(cg + 1) * 256],
                                                                               in1=t1[:, 0:256], op=ALU.add),
                                 r=[b_t1, b_x1[tt]], w=[b_x1[tt]])
                    S.dma("sp", rows[1][:], modscr[:, 4 * D:5 * D], r=[G_mod], w=[b_rows[1]])
                    S.dma("sp", rows[0][:], modscr[:, 3 * D:4 * D], r=[G_mod], w=[b_rows[0]])
                    for tt in range(4):
                        S.op("act", lambda e, tt=tt: e.activation(out=hb[:], in_=x1[:, tt, :], func=AF.Square, accum_out=ss[:, 0:1]),
                             r=[b_x1[tt]], w=[b_hb, b_ss])
                        S.op("act", lambda e: e.activation(out=ss[:, 1:2], in_=ss[:, 0:1], func=AF.Sqrt, scale=1.0 / D, bias=1e-6),
                             r=[b_ss], w=[b_ss])
                        S.op("dve", lambda e: e.reciprocal(ss[:, 2:3], ss[:, 1:2]), r=[b_ss], w=[b_ss])
                        S.op("dve", lambda e, tt=tt: e.scalar_tensor_tensor(out=t1[:], in0=x1[:, tt, :], scalar=ss[:, 2:3],
                                                                            in1=rows[1][:], op0=ALU.mult, op1=ALU.mult),
                             r=[b_x1[tt], b_ss, b_rows[1]], w=[b_t1])
                        S.op("dve", lambda e: e.tensor_tensor(out=hb[:], in0=t1[:], in1=rows[0][:], op=ALU.add),
                             r=[b_t1, b_rows[0]], w=[b_hb])
                        for half in range(2):
                            for q_ in range(8):
                                kc = half * 8 + q_
                                S.op("pe", lambda e, half=half, q_=q_, kc=kc: e.transpose(
                                    pb[half][:, q_ * 128:(q_ + 1) * 128], hb[:, kc * 128:(kc + 1) * 128], ident[:]),
                                    r=[b_hb, G_ident], w=[G_pb[half]])
                            S.op("dve" if half else "act", (lambda e, half=half, tt=tt: e.tensor_copy(
                                hT2[:, half * 8:(half + 1) * 8, tt * 128:(tt + 1) * 128],
                                pb[half][:].rearrange("p (a b) -> p a b", a=8))) if half else
                                (lambda e, half=half, tt=tt: e.activation(
                                    out=hT2[:, half * 8:(half + 1) * 8, tt * 128:(tt + 1) * 128],
                                    in_=pb[half][:].rearrange("p (a b) -> p a b", a=8), func=AF.Copy)),
                                r=[G_pb[half]], w=[b_hT2])
                    w_fi_v = w_fi.rearrange("(kc p) n -> p kc n", p=128)
                    S.dma("pool", wbuf[0][:], w_fi_v[:, :, 0:256], w=[b_wbuf[0]])
                    S.dma("pool", wbuf[1][:], w_fi_v[:, :, HID:HID + 256], w=[b_wbuf[1]])
                    for gi in range(22):
                        sl = (gi % 2) * 2
                        if gi + 1 < 22:
                            S.dma("pool", wbuf[2 - sl][:], w_fi_v[:, :, (gi + 1) * 256:(gi + 2) * 256], w=[b_wbuf[2 - sl]])
                            S.dma("pool", wbuf[3 - sl][:], w_fi_v[:, :, HID + (gi + 1) * 256:HID + (gi + 2) * 256],
                                  w=[b_wbuf[3 - sl]])
                        for hl in range(2):
                            hc = gi * 2 + hl
                            pg, pu = hc % 2, 2 + hc % 2
                            for kc in range(16):
                                S.op("pe", lambda e, pg=pg, sl=sl, kc=kc, hl=hl: e.matmul(
                                    pf[pg][:], lhsT=wbuf[sl][:, kc, hl * 128:(hl + 1) * 128], rhs=hT2[:, kc, :],
                                    start=(kc == 0), stop=(kc == 15)), r=[b_wbuf[sl], b_hT2], w=[G_pf[pg]])
                            for kc in range(16):
                                S.op("pe", lambda e, pu=pu, sl=sl, kc=kc, hl=hl: e.matmul(
                                    pf[pu][:], lhsT=wbuf[sl + 1][:, kc, hl * 128:(hl + 1) * 128], rhs=hT2[:, kc, :],
                                    start=(kc == 0), stop=(kc == 15)), r=[b_wbuf[sl + 1], b_hT2], w=[G_pf[pu]])
                            S.op("act", lambda e, pg=pg: e.activation(out=sg[:], in_=pf[pg][:], func=AF.Silu), r=[G_pf[pg]], w=[b_sg])
                            S.op("dve", lambda e, pu=pu, hc=hc: e.tensor_tensor(out=actT[:, hc, :], in0=sg[:], in1=pf[pu][:],
                                                                               op=ALU.mult), r=[b_sg, G_pf[pu]], w=[b_actT])
                    S.dma("sp", rows[1][:], modscr[:, 5 * D:6 * D], r=[G_mod], w=[b_rows[1]])
                    w_fo_v = w_fo.rearrange("(hc p) n -> p hc n", p=128)
                    S.dma("pool", wfo[0][:], w_fo_v[:, 0:11, 0:256], w=[b_wfo[0]])
                    pn = 0
                    for cg in range(8):
                        for piece in range(4):
                            sl = pn % 2
                            pn += 1
                            nxt = pn
                            if nxt < 32:
                                cgn, pcn = nxt // 4, nxt % 4
                                S.dma("pool", wfo[1 - sl][:], w_fo_v[:, pcn * 11:(pcn + 1) * 11, cgn * 256:(cgn + 1) * 256],
                                      w=[b_wfo[1 - sl]])
                            for hl in range(11):
                                hc = piece * 11 + hl
                                for tt in range(4):
                                    S.op("pe", lambda e, tt=tt, hc=hc, hl=hl, sl=sl: e.matmul(
                                        pf[tt][:, 0:256], lhsT=actT[:, hc, tt * 128:(tt + 1) * 128], rhs=wfo[sl][:, hl, :],
                                        start=(hc == 0), stop=(hc == NHC - 1)), r=[b_actT, b_wfo[sl]], w=[G_pf[tt]])
                        for tt in range(4):
                            S.op("dve", lambda e, tt=tt, cg=cg: e.tensor_tensor(out=t1[:, 0:256], in0=pf[tt][:, 0:256],
                                                                               in1=rows[1][:, cg * 256:(cg + 1) * 256], op=ALU.mult),
                                 r=[G_pf[tt], b_rows[1]], w=[b_t1])
                            S.op("dve", lambda e, tt=tt, cg=cg: e.tensor_tensor(out=x1[:, tt, cg * 256:(cg + 1) * 256],
                                                                               in0=x1[:, tt, cg * 256:(cg + 1) * 256],
                                                                               in1=t1[:, 0:256], op=ALU.add),
                                 r=[b_t1, b_x1[tt]], w=[b_x1[tt]])
                    for tt in range(4):
                        S.dma("sp", y_out[(tb * 4 + tt) * 128:(tb * 4 + tt + 1) * 128, :], x1[:, tt, :], r=[b_x1[tt]], w=[G_y])
                stop = phase_done()

        S.flush(barrier=True)
        if os.environ.get("K_VERBOSE"):
            print("n_inst", S.n_inst, "phases", phase[0], "counts", {str(k): v for k, v in S.cnt.items()})
    return nc


def kernel(**inputs):
    f32 = np.float32
    x = np.asarray(inputs["x"], f32)
    c = np.asarray(inputs["c"], f32)
    nc = build_nc()
    in_maps = []
    consts = [_consts(j) for j in range(4)]
    rep = lambda v: np.ascontiguousarray(np.broadcast_to(np.asarray(v, f32), (128,) + np.asarray(v).shape))
    shared = {
        "w_ada": np.ascontiguousarray(inputs["w_ada"][0]),
        "b_ada": np.ascontiguousarray(inputs["b_ada"][0][None, :]),
        "w_in": np.ascontiguousarray(inputs["w_in"][0]),
        "g_nq": rep(inputs["nsa_q_norm"][0]),
        "g_nk": rep(inputs["nsa_k_norm"][0]),
        "g_mq": rep(inputs["moba_q_norm"][0]),
        "g_mk": rep(inputs["moba_k_norm"][0]),
        "pe_k": np.ascontiguousarray(inputs["cmp_pe_k"][0]),
        "w1_k": np.ascontiguousarray(inputs["cmp_w1_k"][0]),
        "w2_k": np.ascontiguousarray(inputs["cmp_w2_k"][0]),
        "pe_v": np.ascontiguousarray(inputs["cmp_pe_v"][0]),
        "w1_v": np.ascontiguousarray(inputs["cmp_w1_v"][0]),
        "w2_v": np.ascontiguousarray(inputs["cmp_w2_v"][0]),
        "g_out": rep(inputs["out_norm"][0]),
        "w_out": np.ascontiguousarray(inputs["w_out"][0]),
        "w_fi": np.ascontiguousarray(inputs["w_ffn_in"][0]),
        "w_fo": np.ascontiguousarray(inputs["w_ffn_out"][0]),
        "ident32": np.eye(32, dtype=f32),
    }
    shared = {k: np.asarray(v, f32) if v.dtype != ml_dtypes.bfloat16 else v for k, v in shared.items()}
    for b in range(2):
        for j in range(4):
            cj = consts[j]
            m = dict(shared)
            m["x_full"] = np.ascontiguousarray(x[b])
            m["x_own"] = np.ascontiguousarray(x[b][cj["own_tok"]])
            m["c_col"] = np.ascontiguousarray(c[b].reshape(16, 128).T)
            for k in ("cs_full", "cs_own", "cs_cmp", "cmpmask", "cmask", "wmask", "addsel", "addmoba", "xsel",
                      "xmoba", "ident", "ovaug"):
                m[k] = cj[k]
            in_maps.append(m)
    res = run_bass_kernel_spmd(nc, in_maps, core_ids=list(range(8)))
    out = np.zeros((2, T, D), f32)
    for b in range(2):
        for j in range(4):
            r = res.results[4 * b + j]
            out[b][consts[j]["own_tok"]] = r["y_out"]
    if KDEBUG:
        kernel.last = res.results
    return out
```
